# Optimizing a Trainium2 kernel written in Bass

```python
import jax, jax.numpy as jnp
from jax import lax
import numpy as np

D_MODEL = 2048
BATCH = 8
SEQ = 2048
DEPTH = 1

HEAD_DIM = 128
N_MIX_HEADS = D_MODEL // HEAD_DIM
N_MEM_HEADS = 4
N_SGU_HEADS = (N_MIX_HEADS - N_MEM_HEADS) // 2
N_CONV_GROUPS = N_MIX_HEADS - N_MEM_HEADS - N_SGU_HEADS
D_SGU = N_SGU_HEADS * HEAD_DIM
D_CONV = N_CONV_GROUPS * HEAD_DIM
D_MEM = N_MEM_HEADS * HEAD_DIM
D_MIX = D_SGU + D_CONV + D_MEM
D_IN = 2 * D_SGU + 3 * D_CONV + D_MEM
CHUNK = 128
CONV_W = 3
N_MEM = 256
D_FF = 4 * D_MODEL
EPS = 1e-6

kernel_name = "hybrid_sgu_shortconv_memattn_block"


def rms_norm(x, g):
    xf = x.astype(jnp.float32)
    y = xf * lax.rsqrt(jnp.mean(xf * xf, axis=-1, keepdims=True) + EPS)
    return (y * g.astype(jnp.float32)).astype(x.dtype)


def layer_norm(x, g, b):
    xf = x.astype(jnp.float32)
    mu = jnp.mean(xf, axis=-1, keepdims=True)
    xc = xf - mu
    y = xc * lax.rsqrt(jnp.mean(xc * xc, axis=-1, keepdims=True) + EPS)
    return (y * g.astype(jnp.float32) + b.astype(jnp.float32)).astype(x.dtype)


def chunked_spatial_gating(u, v, ln_g, ln_b, w_s, b_s):
    bsz, s, _ = v.shape
    u = jax.nn.gelu(u)
    v = layer_norm(jax.nn.gelu(v), ln_g, ln_b)
    vc = v.reshape(bsz, s // CHUNK, CHUNK, N_SGU_HEADS, HEAD_DIM)
    causal = jnp.tril(jnp.ones((CHUNK, CHUNK), dtype=bool))
    w = jnp.where(causal[None], w_s, jnp.zeros_like(w_s))
    mixed = jnp.einsum('hts,bcshd->bcthd', w, vc) + b_s.T[:, :, None]
    return u * mixed.reshape(bsz, s, D_SGU)


def short_gated_conv(b_gate, c_gate, xin, conv_w):
    xc = c_gate * xin
    y = lax.conv_general_dilated(
        xc, conv_w[:, None, :].astype(xc.dtype),
        window_strides=(1,), padding=[(CONV_W - 1, 0)],
        dimension_numbers=('NWC', 'WIO', 'NWC'),
        feature_group_count=D_CONV)
    return b_gate * y


def memory_attention(q, mem_n, w_kv):
    bsz, s, _ = q.shape
    m = mem_n.shape[1]
    k, v = jnp.split(mem_n @ w_kv, 2, axis=-1)
    q = q.reshape(bsz, s, N_MEM_HEADS, HEAD_DIM) * (HEAD_DIM ** -0.5)
    k = k.reshape(bsz, m, N_MEM_HEADS, HEAD_DIM)
    v = v.reshape(bsz, m, N_MEM_HEADS, HEAD_DIM)
    scores = jnp.einsum('bshd,bmhd->bhsm', q, k).astype(jnp.float32)
    p = jax.nn.softmax(scores, axis=-1).astype(v.dtype)
    o = jnp.einsum('bhsm,bmhd->bshd', p, v)
    return o.reshape(bsz, s, D_MEM)


def setup_inputs(seed: int = 0) -> dict:
    key = jax.random.key(seed)
    ks = jax.random.split(key, 20)
    f32 = jnp.float32
    nrm = lambda k, shape, scale: jax.random.normal(k, shape, f32) * scale
    gain = lambda k, shape: 1.0 + 0.02 * jax.random.normal(k, shape, f32)
    return {
        "x": jax.random.normal(ks[0], (BATCH, SEQ, D_MODEL), f32),
        "mem": jax.random.normal(ks[1], (BATCH, N_MEM, D_MODEL), f32),
        "g_mix": gain(ks[2], (DEPTH, D_MODEL)),
        "w_in": nrm(ks[3], (DEPTH, D_MODEL, D_IN), D_MODEL ** -0.5),
        "ln_v_g": gain(ks[4], (DEPTH, D_SGU)),
        "ln_v_b": nrm(ks[5], (DEPTH, D_SGU), 0.02),
        "w_s": nrm(ks[6], (DEPTH, N_SGU_HEADS, CHUNK, CHUNK), CHUNK ** -0.5),
        "b_s": gain(ks[7], (DEPTH, N_SGU_HEADS, CHUNK)),
        "conv_w": nrm(ks[8], (DEPTH, CONV_W, D_CONV), CONV_W ** -0.5),
        "g_mem": gain(ks[9], (DEPTH, D_MODEL)),
        "w_kv": nrm(ks[10], (DEPTH, D_MODEL, 2 * D_MEM), D_MODEL ** -0.5),
        "g_head": gain(ks[11], (DEPTH, D_MIX)),
        "w_o": nrm(ks[12], (DEPTH, D_MIX, D_MODEL), D_MIX ** -0.5),
        "g_ffn": gain(ks[13], (DEPTH, D_MODEL)),
        "w_ffn1": nrm(ks[14], (DEPTH, D_MODEL, D_FF), D_MODEL ** -0.5),
        "w_ffn2": nrm(ks[15], (DEPTH, D_FF, D_MODEL), D_FF ** -0.5),
        "g_final": gain(ks[16], (D_MODEL,)),
    }


def reference(x, mem, g_mix, w_in, ln_v_g, ln_v_b, w_s, b_s, conv_w, g_mem,
              w_kv, g_head, w_o, g_ffn, w_ffn1, w_ffn2, g_final):
    bsz, s, _ = x.shape
    split_at = np.cumsum([D_SGU, D_SGU, D_CONV, D_CONV, D_CONV])
    for l in range(DEPTH):
        h = rms_norm(x, g_mix[l])
        proj = h @ w_in[l]
        u, v, b_gate, c_gate, xin, q = jnp.split(proj, split_at, axis=-1)
        a_out = chunked_spatial_gating(u, v, ln_v_g[l], ln_v_b[l], w_s[l], b_s[l])
        c_out = short_gated_conv(b_gate, c_gate, xin, conv_w[l])
        m_out = memory_attention(q, rms_norm(mem, g_mem[l]), w_kv[l])
        heads = jnp.concatenate([a_out, c_out, m_out], axis=-1)
        heads = rms_norm(heads.reshape(bsz, s, N_MIX_HEADS, HEAD_DIM),
                         jnp.ones((HEAD_DIM,), heads.dtype)).reshape(bsz, s, D_MIX)
        x = x + (heads * g_head[l]) @ w_o[l]
        f = rms_norm(x, g_ffn[l]) @ w_ffn1[l]
        x = x + jnp.square(jax.nn.relu(f)) @ w_ffn2[l]
    return rms_norm(x, g_final)
```

```python
from contextlib import ExitStack

import numpy as np
import concourse.bass as bass
import concourse.mybir as mybir
from concourse.bass_utils import run_bass_kernel_spmd

F32 = mybir.dt.float32
BF16 = mybir.dt.bfloat16
AF = mybir.ActivationFunctionType
ALU = mybir.AluOpType
AX = mybir.AxisListType

D = 2048
SEQ = 2048
NMEM = 256
D_IN = 4352
DFF = 8192
KC = 16
TS = 1024
NST = SEQ // TS
TPS = TS // 128
NSLOT = 6
EPS = 1e-6
C_U, C_V, C_B, C_C, C_X, C_Q = 0, 768, 1536, 2304, 3072, 3840
GELU_C2 = 1.5957691216057308
ATT_SCALE = 128 ** -0.5

DEBUG = False


class Tok:
    __slots__ = ("sem", "val")

    def __init__(self, sem, val):
        self.sem = sem
        self.val = val


class Slot:
    def __init__(self, sem):
        self.sem = sem
        self.cnt = 0


class Prog:
    ENGS = ("pe", "act", "dve", "pool", "sp")

    def __init__(self, psem):
        self.q = {e: [] for e in self.ENGS}
        self.psem = psem
        self.cnt = {e: 0 for e in self.ENGS}
        self.waited = {e: {} for e in self.ENGS}

    def wait(self, eng, tok):
        if tok is None:
            return
        if isinstance(tok, (list, tuple)):
            for t in tok:
                self.wait(eng, t)
            return
        key = id(tok.sem)
        if self.waited[eng].get(key, 0) >= tok.val:
            return
        self.waited[eng][key] = tok.val
        sem, val = tok.sem, tok.val
        self.q[eng].append(lambda E: E.wait_ge(sem, val))

    def op(self, eng, fn, deps=(), sig=True):
        self.wait(eng, deps)
        if sig:
            self.cnt[eng] += 1
            val = self.cnt[eng]
            sem = self.psem[eng]
            self.q[eng].append(lambda E: fn(E).then_inc(sem, 1))
            return Tok(sem, val)
        self.q[eng].append(fn)
        return None

    def dma(self, eng, out, in_, slot, deps=()):
        self.wait(eng, deps)
        slot.cnt += 16
        val = slot.cnt
        sem = slot.sem
        self.q[eng].append(lambda E: E.dma_start(out=out, in_=in_).then_inc(sem, 16))
        return Tok(sem, val)


def build_nc():
    nc = bass.Bass("TRN2", target_bir_lowering=False)
    dt = lambda name, shape, kind="ExternalInput": nc.dram_tensor(name, shape, F32, kind=kind).ap()
    x_d = dt("x", [SEQ, D])
    mem_d = dt("mem", [NMEM, D])
    g_mix_d = dt("g_mix", [D])
    w_in_d = dt("w_in", [D, D_IN])
    lng_d = dt("ln_v_g", [768])
    lnb_d = dt("ln_v_b", [768])
    wst_d = dt("w_sT", [6, 128, 128])
    bs_d = dt("b_s", [768])
    cw_d = dt("conv_wp", [128, 18])
    g_mem_d = dt("g_mem", [D])
    w_kv_d = dt("w_kv", [D, 1024])
    gh_d = dt("g_headp", [128, 16])
    w_o_d = dt("w_o", [D, D])
    g_ffn_d = dt("g_ffn", [D])
    w1_d = dt("w_ffn1", [D, DFF])
    w2_d = dt("w_ffn2", [DFF, D])
    g_fin_d = dt("g_final", [D])
    out_d = dt("out", [SEQ, D], kind="ExternalOutput")
    if DEBUG:
        dbg_d = dt("dbg", [128, 16 * 1024], kind="ExternalOutput")

    with ExitStack() as es:
        sb = lambda name, shape, dty: es.enter_context(nc.sbuf_tensor(name, shape, dty))
        R1 = sb("R1", [128, 16384], F32)
        actT = sb("actT", [128, KC, TS], BF16)
        headsT = sb("headsT", [128, KC, TS], BF16)
        ring = [sb(f"ring{i}", [128, 4096], BF16) for i in range(NSLOT)]
        Gt = sb("G", [128, D], F32)
        biasT = sb("biasT", [128, 6, 128], F32)
        WsT = sb("WsT", [128, 6, 128], BF16)
        KT = sb("KT", [128, 4, 256], BF16)
        Vm = sb("Vm", [128, 2, 512], BF16)
        ident = sb("ident", [128, 128], BF16)
        onesM = sb("onesM", [128, 128], BF16)
        cw = sb("cw", [128, 18], F32)
        gh = sb("gh", [128, 16], F32)
        hist = sb("hist", [128, 12], F32)
        ss = sb("ss", [128, 512], F32)
        hb = [sb(f"hb{i}", [128, D], BF16) for i in range(2)]
        junk = sb("junk", [128, D], BF16)
        hb_free = [[], []]
        junk_tok = [[]]
        ps = [es.enter_context(nc.psum_tensor(f"ps{i}", [128, 512], F32)) for i in range(8)]

        sem = lambda name: es.enter_context(nc.semaphore(name))
        psem = {e: sem(f"p_{e}") for e in Prog.ENGS}
        P = Prog(psem)
        ring_slots = [Slot(sem(f"ring{i}")) for i in range(NSLOT)]
        xs_slots = [Slot(sem(f"xs{i}")) for i in range(4)]
        xr_slots = [Slot(sem(f"xr{i}")) for i in range(TPS)]
        g_slot = Slot(sem("g"))
        ln_slot = Slot(sem("ln"))
        c_slot = Slot(sem("consts"))
        o_slots = [Slot(sem(f"outst{i}")) for i in range(TPS)]

        xres = R1[:, :].rearrange("p (t d) -> p t d", t=TPS)
        xs = [R1[:, i * 2048:(i + 1) * 2048] for i in range(4)]
        v_n = R1[:, 4096:4096 + 3072].bitcast(BF16).rearrange("p (t d) -> p t d", t=TPS)
        SC0 = 4096 + 3072
        scr = [R1[:, SC0 + i * 512: SC0 + (i + 1) * 512] for i in range(8)]
        VG0 = SC0 + 8 * 512
        vg = [R1[:, VG0: VG0 + 768], R1[:, VG0 + 768: VG0 + 1536]]
        XC0 = VG0 + 1536
        xcb = R1[:, XC0: XC0 + 516]
        LNG = R1[:, 13440:13440 + 768]
        LNB = R1[:, 14208:14208 + 768]
        gT = [headsT[:, 0:4, :], headsT[:, 4:8, :]]
        rsc = [headsT[:, 8 + i, :].bitcast(F32) for i in range(2)]

        bank_free = [[] for _ in range(8)]
        bank_rr = [0]

        held = set()

        def get_bank():
            for _ in range(8):
                b = bank_rr[0]
                bank_rr[0] = (b + 1) % 8
                if b not in held:
                    break
            else:
                raise RuntimeError("all PSUM banks held")
            held.add(b)
            toks = bank_free[b]
            bank_free[b] = []
            return b, toks

        def rel(b, toks):
            bank_free[b] += list(toks)
            held.discard(b)

        pieces = []

        def plan():
            def win(name, c0, w):
                pieces.append((name, [(0, w_in_d[:, c0:c0 + w].rearrange("(kc p) c -> p kc c", p=128), KC, w)]))
            for st in range(NST):
                for j in range(3):
                    win(f"v{st}_{j}", C_V + j * 256, 256)
                if st == 0:
                    for j in range(4):
                        pieces.append((f"kv{j}", [(0, w_kv_d[:, j * 256:(j + 1) * 256].rearrange("(kc p) c -> p kc c", p=128), KC, 256)]))
                for j in range(2):
                    win(f"q{st}_{j}", C_Q + j * 256, 256)
                for j in range(3):
                    win(f"u{st}_{j}", C_U + j * 256, 256)
                for gp in range(3):
                    for k, c0 in enumerate((C_B, C_C, C_X)):
                        win(f"cv{st}_{gp}_{k}", c0 + gp * 256, 256)
                for j in range(8):
                    pieces.append((f"wo{st}_{j}", [(0, w_o_d[:, j * 256:(j + 1) * 256].rearrange("(kc p) c -> p kc c", p=128), KC, 256)]))
                nfc = DFF // 512
                def f1(fc):
                    for p_ in range(2):
                        c0 = fc * 512 + p_ * 256
                        pieces.append((f"w1_{st}_{fc}_{p_}", [(0, w1_d[:, c0:c0 + 256].rearrange("(kc p) c -> p kc c", p=128), KC, 256)]))
                def f2(fc):
                    for p_ in range(2):
                        r0 = fc * 512 + p_ * 256
                        pieces.append((f"w2_{st}_{fc}_{p_}", [(0, w2_d[r0:r0 + 256, :].rearrange("(fb p) c -> p fb c", p=128), 2, D)]))
                f1(0)
                for fc in range(nfc):
                    if fc + 1 < nfc:
                        f1(fc + 1)
                    f2(fc)

        plan()
        ring_state = {"issued": 0, "taken": 0}
        slot_release = [[] for _ in range(NSLOT)]
        piece_tok = {}

        def ring_issue():
            while ring_state["issued"] < len(pieces) and ring_state["issued"] < ring_state["released"] + NSLOT:
                i = ring_state["issued"]
                name, segs = pieces[i]
                s = i % NSLOT
                P.wait("pool", slot_release[s])
                slot_release[s] = []
                tok = None
                for (off, src, a, b) in segs:
                    dst = ring[s][:, off:off + a * b].rearrange("p (a b) -> p a b", a=a)
                    tok = P.dma("pool", dst, src, ring_slots[s])
                piece_tok[i] = tok
                ring_state["issued"] += 1

        ring_state["released"] = 0

        def ring_take(name):
            i = ring_state["taken"]
            assert pieces[i][0] == name, (pieces[i][0], name)
            assert i < ring_state["issued"], "piece not issued (lookahead too small)"
            ring_state["taken"] += 1
            return i, ring[i % NSLOT], piece_tok[i]

        def ring_release(i, toks):
            slot_release[i % NSLOT] = list(toks)
            ring_state["released"] += 1
            ring_issue()

        stat_col = [0]

        def new_stat(n=1):
            c = stat_col[0]
            stat_col[0] += n
            assert stat_col[0] <= 512
            return c

        def stat_reset(deps):
            stat_col[0] = 0

        def rstd_from_sum(col_in, col_out, inv_n, deps):
            t1 = P.op("act", lambda E: E.activation(out=ss[:, col_out:col_out + 1], in_=ss[:, col_in:col_in + 1],
                                                     func=AF.Ln, bias=EPS, scale=inv_n), deps)
            t2 = P.op("act", lambda E: E.activation(out=ss[:, col_out:col_out + 1], in_=ss[:, col_out:col_out + 1],
                                                     func=AF.Exp, scale=-0.5), [t1])
            return t2

        def mm_group(out_ap, pairs, deps, bank_toks):
            P.wait("pe", deps)
            P.wait("pe", bank_toks)
            n = len(pairs)
            tok = None
            for i, (l, r) in enumerate(pairs):
                last = i == n - 1
                fn = (lambda l=l, r=r, st=(i == 0), sp_=last: (lambda E: E.matmul(out_ap, lhsT=l, rhs=r, start=st, stop=sp_)))()
                tok = P.op("pe", fn, (), sig=last)
            return tok

        def transpose16(src_bf, dst_fn, deps, evac_engs=("act", "dve"), defer=False):
            pend = []
            for half in range(2):
                b, btoks = get_bank()
                pview = ps[b][:, :].bitcast(BF16)
                P.wait("pe", deps)
                P.wait("pe", btoks)
                t = None
                for j in range(8):
                    kc = half * 8 + j
                    fn = (lambda kc=kc, j=j, pview=pview: (lambda E: E.transpose(pview[:, j * 128:(j + 1) * 128], src_bf[:, kc * 128:(kc + 1) * 128], ident[:, :])))()
                    t = P.op("pe", fn, (), sig=(j == 7))
                pend.append((half, b, pview, t))

            def evac():
                toks = []
                for (half, b, pview, t) in pend:
                    eng = evac_engs[half % len(evac_engs)]
                    dst = dst_fn(half)
                    src = pview.rearrange("p (a b) -> p a b", a=8)
                    if eng == "act":
                        te = P.op("act", lambda E, dst=dst, src=src: E.activation(out=dst, in_=src, func=AF.Copy), [t])
                    else:
                        te = P.op("dve", lambda E, dst=dst, src=src: E.tensor_copy(out=dst, in_=src), [t])
                    rel(b, [te])
                    toks.append(te)
                return toks
            if defer:
                return [p_[3] for p_ in pend], evac
            return evac()

        def rmsnorm_tile(src, G, hbuf, deps, hbuf_free):
            c = new_stat(2)
            t0 = P.op("act", lambda E: E.activation(out=hbuf[:, :], in_=src, func=AF.Square, accum_out=ss[:, c:c + 1]),
                      list(deps) + list(hbuf_free))
            t1 = rstd_from_sum(c, c + 1, 1.0 / D, [t0])
            t2 = P.op("dve", lambda E: E.scalar_tensor_tensor(out=hbuf[:, :], in0=src, scalar=ss[:, c + 1:c + 2], in1=G,
                                                               op0=ALU.mult, op1=ALU.mult), [t1, t0] + list(deps))
            return t2

        def gelu(src_ps, xs_t, tmp_a, tmp_b, out_t, deps, n):
            t_x = P.op("act", lambda E: E.activation(out=xs_t, in_=src_ps, func=AF.Copy), deps)
            t_s = P.op("act", lambda E: E.activation(out=tmp_a, in_=src_ps, func=AF.Square), deps)
            t_w = P.op("dve", lambda E: E.tensor_scalar(out=tmp_a, in0=tmp_a, scalar1=0.044715 * GELU_C2, scalar2=GELU_C2,
                                                         op0=ALU.mult, op1=ALU.add), [t_s])
            t_z = P.op("dve", lambda E: E.tensor_tensor(out=tmp_a, in0=tmp_a, in1=xs_t, op=ALU.mult), [t_w, t_x])
            t_e = P.op("act", lambda E: E.activation(out=tmp_b, in_=tmp_a, func=AF.Exp, scale=-1.0), [t_z])
            t_d = P.op("dve", lambda E: E.tensor_scalar(out=tmp_b, in0=tmp_b, scalar1=1.0, scalar2=None, op0=ALU.add), [t_e])
            t_r = P.op("dve", lambda E: E.reciprocal(out=tmp_b, in_=tmp_b), [t_d])
            t_g = P.op("dve", lambda E: E.tensor_tensor(out=out_t, in0=xs_t, in1=tmp_b, op=ALU.mult), [t_r, t_x])
            return t_g, [t_x, t_s]

        def head_rms(a_t, sq_bf, rs_t, blk, tb, deps):
            t_sq = P.op("act", lambda E: E.activation(out=sq_bf, in_=a_t, func=AF.Square), deps)
            b, btoks = get_bank()
            t_mm = mm_group(ps[b][:, :], [(onesM[:, :], sq_bf)], [t_sq], btoks)
            t_l = P.op("act", lambda E: E.activation(out=rs_t, in_=ps[b][:, :], func=AF.Ln, bias=EPS), [t_mm])
            rel(b, [t_l])
            t_r = P.op("act", lambda E: E.activation(out=rs_t, in_=rs_t, func=AF.Exp, scale=-0.5), [t_l])
            dst = headsT[:, blk, tb * 512:(tb + 1) * 512]
            t_h = P.op("dve", lambda E: E.scalar_tensor_tensor(out=dst, in0=a_t, scalar=gh[:, blk:blk + 1], in1=rs_t,
                                                                op0=ALU.mult, op1=ALU.mult), [t_r] + list(deps))
            return t_h

        def wcols(slot_t, a, b):
            return slot_t[:, 0:a * b].rearrange("p (a b) -> p a b", a=a)

        def run_sched(units):
            items = []
            for n, stages in enumerate(units):
                for si, (off, cls, fn) in enumerate(stages):
                    items.append((n + off, cls, n, si, fn))
            items.sort(key=lambda it: it[:4])
            for it in items:
                it[4]()

        def norm_unit(u, src, G, hslot, deps_fn, g_use, gdeps_fn=None):
            def s_act():
                c = new_stat(2)
                u["c"] = c
                t0 = P.op("act", lambda E: E.activation(out=junk[:, :], in_=src, func=AF.Square, accum_out=ss[:, c:c + 1]), deps_fn() + junk_tok[0])
                junk_tok[0] = [t0]
                u["t0"] = t0
                u["t1"] = rstd_from_sum(c, c + 1, 1.0 / D, [t0])

            def s_dve():
                c = u["c"]
                t2 = P.op("dve", lambda E: E.scalar_tensor_tensor(out=hb[hslot][:, :], in0=src, scalar=ss[:, c + 1:c + 2], in1=G,
                                                                   op0=ALU.mult, op1=ALU.mult), [u["t1"], u["t0"]] + deps_fn() + (gdeps_fn() if gdeps_fn else []) + hb_free[hslot])
                u["th"] = t2
                g_use.append(t2)
            return s_act, s_dve

        c_toks = []
        c_toks.append(P.dma("sp", biasT[:, :, :], bs_d.partition_broadcast(128).rearrange("p (h t) -> p h t", h=6), c_slot))
        wst_stage = R1[:, 14976:14976 + 768].rearrange("p (h t) -> p h t", h=6)
        c_toks.append(P.dma("sp", wst_stage, wst_d.rearrange("h s t -> s h t"), c_slot))
        c_toks.append(P.dma("sp", cw[:, :], cw_d, c_slot))
        c_toks.append(P.dma("sp", gh[:, :], gh_d, c_slot))
        c_all = c_toks[-1]

        id_stage = R1[:, 15744:15744 + 128]
        t_i0 = P.op("pool", lambda E: E.memset(id_stage, 0.0))
        t_i1 = P.op("pool", lambda E: E.affine_select(out=id_stage, in_=id_stage, pattern=[[-1, 128]], compare_op=ALU.not_equal,
                                                       fill=1.0, base=0, channel_multiplier=1), [t_i0])
        t_id = P.op("pool", lambda E: E.tensor_copy(out=ident[:, :], in_=id_stage), [t_i1])
        t_on = P.op("pool", lambda E: E.memset(onesM[:, :], 1.0 / 128))
        t_h0 = P.op("pool", lambda E: E.memset(hist[:, :], 0.0))
        t_s0 = P.op("pool", lambda E: E.memset(ss[:, :], 0.0))
        t_w0 = P.op("pool", lambda E: E.affine_select(out=wst_stage, in_=wst_stage, pattern=[[0, 6], [1, 128]], compare_op=ALU.is_ge,
                                                       fill=0.0, base=0, channel_multiplier=-1), [c_all])
        t_ws = P.op("pool", lambda E: E.tensor_copy(out=WsT[:, :, :], in_=wst_stage), [t_w0])
        setup_toks = [t_id, t_on, t_h0, t_s0, t_ws, c_all]
        for e_ in ("pe", "act", "dve"):
            P.wait(e_, setup_toks)

        ring_issue()

        g_last_use = []
        kv_readers = []
        actT_free = []

        r1_free = []
        out_toks = []
        prev_st_done = []
        prev_store = {}
        heads_free = []
        for st in range(NST):
            tok0 = st * TS
            stat_col[0] = 0
            p1s = {}
            xs_free = [[], [], [], []]
            hT_toks = []
            g_use = []
            p1_units = []
            hT_by_tile = {}

            def make_p1_unit(tt):
                u = {}
                xsl = tt % 4
                hsl = tt % 2

                def s_dma():
                    ov = {0: (0,), 1: (1,), 2: (3, 4), 3: (4, 5)}[xsl]
                    pdeps = [prev_store[o_] for o_ in ov] if (tt < 4 and prev_store) else []
                    u["tx"] = P.dma("sp", xs[xsl], x_d[tok0 + tt * 128: tok0 + (tt + 1) * 128, :], xs_slots[xsl],
                                    xs_free[xsl] + pdeps + (r1_free if tt < 4 else []))
                    if tt == 2:
                        p1s["tGm"] = P.dma("sp", Gt[:, :], g_mix_d.partition_broadcast(128), g_slot, g_last_use)
                s_act, s_dve0 = norm_unit(u, xs[xsl], Gt[:, :], hsl, lambda: [u["tx"]], g_use, gdeps_fn=lambda: [p1s["tGm"]])

                def s_dve():
                    s_dve0()
                    xs_free[xsl] = [u["th"]]

                def s_pe():
                    pe_toks, ev = transpose16(hb[hsl], lambda half: actT[:, half * 8:(half + 1) * 8, tt * 128:(tt + 1) * 128],
                                              [u["th"]] + actT_free, defer=True)
                    hb_free[hsl] = list(pe_toks)
                    u["ev"] = ev

                def s_ev():
                    tts = u["ev"]()
                    hT_toks.extend(tts)
                    hT_by_tile[tt] = tts
                return [(0, 0, s_dma), (1, 1, s_act), (2, 1, s_dve), (3, 1, s_pe), (4, 1, s_ev)]

            for tt in range(TPS):
                p1_units.append(make_p1_unit(tt))
            run_sched(p1_units)
            g_last_use = g_use
            ln_toks = [P.dma("sp", LNG, lng_d.partition_broadcast(128), ln_slot, prev_st_done + r1_free),
                       P.dma("sp", LNB, lnb_d.partition_broadcast(128), ln_slot, prev_st_done + r1_free)]
            ln_tok = ln_toks[-1]
            actT_free = []
            ph2_toks = []

            K1, K2 = 0.044715 * GELU_C2, GELU_C2

            def sigmoid_chain(T, deps):
                t1 = P.op("act", lambda E: E.activation(out=T, in_=T, func=AF.Exp, scale=-1.0), deps)
                t2 = P.op("act", lambda E: E.activation(out=T, in_=T, func=AF.Ln, bias=1.0), [t1])
                t3 = P.op("act", lambda E: E.activation(out=T, in_=T, func=AF.Exp, scale=-1.0), [t2])
                return t3

            all_xs_free = [t for l_ in xs_free for t in l_] + hT_toks[-2:] + list(prev_st_done)
            VB = [0, 768, 1536, 2304, 3072, 7168, 7936, 8704, 9472, 10240]
            vbuf_free = [list(all_xs_free) for _ in VB]
            vb_rr = [0]

            def vbuf():
                k = vb_rr[0] % len(VB)
                vb_rr[0] += 1
                fr = vbuf_free[k]
                vbuf_free[k] = None
                return k, R1[:, VB[k]:VB[k] + 768], fr

            vp = [ring_take(f"v{st}_{j}") for j in range(3)]
            vn_tok = {}

            def make_v_unit(tt):
                u = {}

                def s0():
                    u["banks"] = []
                    for (c0, wd, pcs) in ((0, 512, (0, 1)), (512, 256, (2,))):
                        b, btoks = get_bank()
                        last = None
                        for pj in pcs:
                            w = wcols(vp[pj][1], KC, 256)
                            o_ap = ps[b][:, (pj * 256 - c0):(pj * 256 - c0) + 256]
                            last = mm_group(o_ap, [(actT[:, kc, tt * 128:(tt + 1) * 128], w[:, kc, :]) for kc in range(KC)],
                                            [vp[pj][2]] + hT_toks, btoks if pj == pcs[0] else [])
                        u["banks"].append((b, c0, wd, last))
                    if tt == TPS - 1:
                        for j in range(3):
                            ring_release(vp[j][0], [u["banks"][-1][3]])

                def s1():
                    kx, Tx, fx = vbuf()
                    u["kx"], u["Tx"] = kx, Tx
                    tx = []
                    for (b, c0, wd, last) in u["banks"]:
                        t1 = P.op("act", lambda E, b=b, c0=c0, wd=wd: E.activation(out=Tx[:, c0:c0 + wd], in_=ps[b][:, 0:wd], func=AF.Copy), [last] + fx)
                        rel(b, [t1])
                        tx.append(t1)
                    u["tx"] = tx

                def s1p():
                    ka, Ta, fa = vbuf()
                    u["ka"], u["Ta"] = ka, Ta
                    Tx = u["Tx"]
                    u["ts"] = [P.op("pool", lambda E: E.tensor_tensor(out=Ta, in0=Tx, in1=Tx, op=ALU.mult), u["tx"] + fa)]

                def s2():
                    Tx, Ta = u["Tx"], u["Ta"]
                    t_w = P.op("dve", lambda E: E.tensor_scalar(out=Ta, in0=Ta, scalar1=K1, scalar2=K2, op0=ALU.mult, op1=ALU.add), u["ts"])
                    u["tz"] = P.op("dve", lambda E: E.tensor_tensor(out=Ta, in0=Ta, in1=Tx, op=ALU.mult), [t_w] + u["tx"])

                def s3():
                    u["tr"] = sigmoid_chain(u["Ta"], [u["tz"]])

                def s4():
                    Tx, Ta = u["Tx"], u["Ta"]
                    c = new_stat(6)
                    u["c"] = c
                    u["tg"] = P.op("dve", lambda E: E.scalar_tensor_tensor(out=Ta, in0=Tx, scalar=1.0, in1=Ta, op0=ALU.mult, op1=ALU.mult,
                                                                            accum_out=ss[:, c:c + 1]), [u["tr"]] + u["tx"])
                    u["s1"] = u["tg"]
                    vbuf_free[u["kx"]] = [u["tg"]]

                def s5():
                    Ta = u["Ta"]
                    c = u["c"]
                    jk = junk[:, 0:768]
                    u["s2"] = P.op("act", lambda E: E.activation(out=jk, in_=Ta, func=AF.Square, accum_out=ss[:, c + 1:c + 2]), [u["tg"]] + junk_tok[0])
                    junk_tok[0] = [u["s2"]]

                def s6():
                    c = u["c"]
                    t_m = P.op("dve", lambda E: E.tensor_scalar(out=ss[:, c + 2:c + 3], in0=ss[:, c:c + 1], scalar1=1.0 / 768, scalar2=None, op0=ALU.mult), [u["s1"]])
                    t_q = P.op("dve", lambda E: E.tensor_tensor(out=ss[:, c + 3:c + 4], in0=ss[:, c + 2:c + 3], in1=ss[:, c + 2:c + 3], op=ALU.mult), [t_m])
                    u["tm"] = t_m
                    u["tv"] = P.op("dve", lambda E: E.scalar_tensor_tensor(out=ss[:, c + 3:c + 4], in0=ss[:, c + 1:c + 2], scalar=1.0 / 768, in1=ss[:, c + 3:c + 4],
                                                                            op0=ALU.mult, op1=ALU.subtract), [t_q, u["s2"]])

                def s7():
                    c = u["c"]
                    t_l = P.op("act", lambda E: E.activation(out=ss[:, c + 4:c + 5], in_=ss[:, c + 3:c + 4], func=AF.Ln, bias=EPS), [u["tv"]])
                    u["trs"] = P.op("act", lambda E: E.activation(out=ss[:, c + 4:c + 5], in_=ss[:, c + 4:c + 5], func=AF.Exp, scale=-0.5), [t_l])

                def s8():
                    c = u["c"]
                    u["tn"] = P.op("dve", lambda E: E.scalar_tensor_tensor(out=ss[:, c + 5:c + 6], in0=ss[:, c + 2:c + 3], scalar=-1.0, in1=ss[:, c + 4:c + 5],
                                                                            op0=ALU.mult, op1=ALU.mult), [u["trs"], u["tm"]])

                def s9():
                    c = u["c"]
                    Ta = u["Ta"]
                    u["tvh"] = P.op("act", lambda E: E.activation(out=Ta, in_=Ta, func=AF.Identity, scale=ss[:, c + 4:c + 5], bias=ss[:, c + 5:c + 6]),
                                    [u["tn"], u["trs"], u["s2"], u["s1"]])

                def s10():
                    Ta = u["Ta"]
                    t_a = P.op("dve", lambda E: E.tensor_tensor(out=Ta, in0=Ta, in1=LNG, op=ALU.mult), [u["tvh"], ln_tok])
                    t_b = P.op("dve", lambda E: E.tensor_tensor(out=v_n[:, tt, :], in0=Ta, in1=LNB, op=ALU.add), [t_a, ln_tok] + all_xs_free)
                    vbuf_free[u["ka"]] = [t_b]
                    vn_tok[tt] = t_b
                    ph2_toks.append(t_b)

                return [(0, 6, s0), (0, 8, s1), (1, 1, s1p), (2, 4, s2), (3, 1, s3), (4, 2, s4), (4, 5, s5),
                        (5, 2, s6), (5, 3, s7), (5, 4, s8), (5, 5, s9), (5, 7, s10)]

            USLOT = [11264, 9216, 0, 2048, 7168]
            NUS = len(USLOT)
            uslot_free = [None] * NUS
            hist_tok = [list(setup_toks)]
            ring_ctx = {}
            UNIT0 = TPS + 6

            def slot_free(n):
                k = (n - UNIT0) % NUS
                if uslot_free[k] is None:
                    lo, hi = USLOT[k], USLOT[k] + 2048
                    toks = list(all_xs_free)
                    for j_, vb0 in enumerate(VB):
                        if vb0 < hi and vb0 + 768 > lo:
                            assert vbuf_free[j_] is not None
                            toks += vbuf_free[j_]
                    uslot_free[k] = toks
                return uslot_free[k]

            def slot_views(n):
                base = USLOT[(n - UNIT0) % NUS]
                T3 = R1[:, base:base + 512]
                T0 = R1[:, base + 512:base + 1024]
                T1 = R1[:, base + 1024:base + 1536]
                T2 = R1[:, base + 1536:base + 2048]
                win = R1[:, base + 510:base + 1024]
                return T0, T1, T2, T3, win

            def c_stages(u, n, a_t, blk, tb, T0, T3, off):
                def c_pe():
                    sq_bf = T0[:, 0:256].bitcast(BF16)
                    b, btoks = get_bank()
                    u["cb"] = b
                    u["cmm"] = mm_group(ps[b][:, :], [(onesM[:, :], sq_bf)], [u["tsq"]] + setup_toks, btoks)

                def c_act():
                    b = u["cb"]
                    t_l = P.op("act", lambda E: E.activation(out=T3, in_=ps[b][:, :], func=AF.Ln, bias=EPS), [u["cmm"]])
                    rel(b, [t_l])
                    u["crs"] = P.op("act", lambda E: E.activation(out=T3, in_=T3, func=AF.Exp, scale=-0.5), [t_l])

                def c_dve():
                    dst = headsT[:, blk, tb * 512:(tb + 1) * 512]
                    t_h = P.op("dve", lambda E: E.scalar_tensor_tensor(out=dst, in0=a_t, scalar=gh[:, blk:blk + 1], in1=T3,
                                                                        op0=ALU.mult, op1=ALU.mult), [u["crs"], u["ta"]] + heads_free + setup_toks)
                    uslot_free[(n - UNIT0) % NUS] = [t_h]
                    ph2_toks.append(t_h)
                return [(off, 0, c_pe), (off, 3, c_act), (off, 7, c_dve)]

            def make_u_unit(n, hd, tb):
                u = {}
                T0, T1, T2, T3, win = slot_views(n)
                j, h2 = hd // 2, hd % 2

                def s0():
                    if h2 == 0 and tb == 0:
                        ring_ctx["u"] = ring_take(f"u{st}_{j}")
                    i, slot_t, ltok = ring_ctx["u"]
                    w = wcols(slot_t, KC, 256)
                    b, btoks = get_bank()
                    t_mm = mm_group(ps[b][:, :], [(w[:, kc, h2 * 128:(h2 + 1) * 128], actT[:, kc, tb * 512:(tb + 1) * 512]) for kc in range(KC)],
                                    [ltok] + hT_toks, btoks)
                    u["b"], u["mm"] = b, t_mm
                    if h2 == 1 and tb == 1:
                        ring_release(i, [t_mm])

                def s1():
                    sfree = slot_free(n)
                    b = u["b"]
                    t1 = P.op("act", lambda E: E.activation(out=T0, in_=ps[b][:, :], func=AF.Copy), [u["mm"]] + sfree)
                    rel(b, [t1])
                    u["tx"] = t1

                def s1p():
                    u["ts"] = P.op("pool", lambda E: E.tensor_tensor(out=T1, in0=T0, in1=T0, op=ALU.mult), [u["tx"]] + slot_free(n))

                def s2():
                    t_w = P.op("dve", lambda E: E.tensor_scalar(out=T1, in0=T1, scalar1=K1, scalar2=K2, op0=ALU.mult, op1=ALU.add), [u["ts"]])
                    u["tz"] = P.op("dve", lambda E: E.tensor_tensor(out=T1, in0=T1, in1=T0, op=ALU.mult), [t_w, u["tx"]])

                def s3():
                    u["tr"] = sigmoid_chain(T1, [u["tz"]])
                    b2_, b2toks = get_bank()
                    u["b2"] = b2_
                    P.wait("pe", b2toks)
                    t_sg = None
                    for cidx in range(4):
                        ch = tb * 4 + cidx
                        fn = (lambda ch=ch, cidx=cidx: (lambda E: E.matmul(ps[b2_][:, cidx * 128:(cidx + 1) * 128],
                                                                          lhsT=v_n[:, ch, hd * 128:(hd + 1) * 128], rhs=WsT[:, hd, :], start=True, stop=True)))()
                        t_sg = P.op("pe", fn, [vn_tok[ch]] + setup_toks, sig=(cidx == 3))
                    u["tsg"] = t_sg

                def s4():
                    b2_ = u["b2"]
                    t_g = P.op("dve", lambda E: E.tensor_tensor(out=T1, in0=T0, in1=T1, op=ALU.mult), [u["tr"], u["tx"]])
                    bias_b = biasT[:, hd:hd + 1, :].to_broadcast([128, 4, 128])
                    t_t = P.op("dve", lambda E: E.tensor_tensor(out=T2.rearrange("p (a b) -> p a b", a=4),
                                                                in0=ps[b2_][:, :].rearrange("p (a b) -> p a b", a=4), in1=bias_b, op=ALU.add),
                               [u["tsg"]] + slot_free(n) + setup_toks)
                    rel(b2_, [t_t])
                    u["ta"] = P.op("dve", lambda E: E.tensor_tensor(out=T2, in0=T2, in1=T1, op=ALU.mult), [t_t, t_g])

                def s5():
                    u["tsq"] = P.op("pool", lambda E: E.tensor_tensor(out=T0[:, 0:256].bitcast(BF16), in0=T2, in1=T2, op=ALU.mult), [u["ta"]])

                return [(0, 6, s0), (0, 8, s1), (1, 1, s1p), (2, 4, s2), (3, 1, s3), (4, 2, s4), (4, 5, s5)] + c_stages(u, n, T2, hd, tb, T0, T3, 5)

            def make_c_unit(n, g, tb):
                u = {}
                T0, T1, T2, T3, win = slot_views(n)
                gp, g2 = g // 2, g % 2

                def s0():
                    if g2 == 0 and tb == 0:
                        ring_ctx["cv"] = [ring_take(f"cv{st}_{gp}_{k_}") for k_ in range(3)]
                    cvp = ring_ctx["cv"]
                    bks, mms = [], []
                    for s_ in range(3):
                        w = wcols(cvp[s_][1], KC, 256)
                        b, btoks = get_bank()
                        t_mm = mm_group(ps[b][:, :], [(w[:, kc, g2 * 128:(g2 + 1) * 128], actT[:, kc, tb * 512:(tb + 1) * 512]) for kc in range(KC)],
                                        [cvp[s_][2]] + hT_toks, btoks)
                        bks.append(b)
                        mms.append(t_mm)
                    u["bks"], u["mms"] = bks, mms
                    if g2 == 1 and tb == 1:
                        for c_ in cvp:
                            ring_release(c_[0], [mms[-1]])

                def s1():
                    sfree = slot_free(n)
                    bB, bC, bX = u["bks"]
                    mB, mC, mX = u["mms"]
                    t_c = P.op("act", lambda E: E.activation(out=T1, in_=ps[bC][:, :], func=AF.Copy), [mC] + sfree)
                    rel(bC, [t_c])
                    t_b = P.op("act", lambda E: E.activation(out=T2, in_=ps[bB][:, :], func=AF.Copy), [mB] + sfree)
                    rel(bB, [t_b])
                    u["tb"], u["tc"] = t_b, t_c

                def s1b():
                    sfree = slot_free(n)
                    bB, bC, bX = u["bks"]
                    mB, mC, mX = u["mms"]
                    t_hc = P.op("dve", lambda E: E.tensor_copy(out=T3[:, 510:512], in_=hist[:, 2 * g:2 * g + 2]), sfree + hist_tok[0])
                    t_xc = P.op("dve", lambda E: E.tensor_tensor(out=T0, in0=T1, in1=ps[bX][:, :], op=ALU.mult), [u["tc"], mX] + sfree)
                    rel(bX, [t_xc])
                    t_hs = P.op("dve", lambda E: E.tensor_copy(out=hist[:, 2 * g:2 * g + 2], in_=T0[:, 510:512]), [t_xc, t_hc])
                    hist_tok[0] = [t_hs]
                    u["txc"], u["thc"] = t_xc, t_hc

                def s2():
                    t_y0 = P.op("dve", lambda E: E.tensor_scalar(out=T1, in0=T0, scalar1=cw[:, 3 * g + 2:3 * g + 3], scalar2=None, op0=ALU.mult), [u["txc"], u["thc"]] + setup_toks)
                    t_y1 = P.op("dve", lambda E: E.scalar_tensor_tensor(out=T1, in0=win[:, 1:513], scalar=cw[:, 3 * g + 1:3 * g + 2], in1=T1,
                                                                         op0=ALU.mult, op1=ALU.add), [t_y0])
                    t_y2 = P.op("dve", lambda E: E.scalar_tensor_tensor(out=T1, in0=win[:, 0:512], scalar=cw[:, 3 * g:3 * g + 1], in1=T1,
                                                                         op0=ALU.mult, op1=ALU.add), [t_y1])
                    u["ta"] = P.op("dve", lambda E: E.tensor_tensor(out=T1, in0=T1, in1=T2, op=ALU.mult), [t_y2, u["tb"]])

                def s3():
                    u["tsq"] = P.op("act", lambda E: E.activation(out=T0[:, 0:256].bitcast(BF16), in_=T1, func=AF.Square), [u["ta"]])

                return [(0, 6, s0), (0, 8, s1), (0, 9, s1b), (1, 2, s2), (1, 5, s3)] + c_stages(u, n, T1, 6 + g, tb, T0, T3, 2)

            def make_a_unit(n, hh, tb):
                u = {}
                T0, T1, T2, T3, win = slot_views(n)
                j, h2 = hh // 2, hh % 2
                qT = T0[:, 0:256].bitcast(BF16)
                p_bf = T3.bitcast(BF16)
                pT = T1.bitcast(BF16)
                eT = [T1, T2]

                def s0():
                    if h2 == 0 and tb == 0:
                        ring_ctx["q"] = ring_take(f"q{st}_{j}")
                    i, slot_t, ltok = ring_ctx["q"]
                    w = wcols(slot_t, KC, 256)
                    b, btoks = get_bank()
                    t_mm = mm_group(ps[b][:, :], [(w[:, kc, h2 * 128:(h2 + 1) * 128], actT[:, kc, tb * 512:(tb + 1) * 512]) for kc in range(KC)],
                                    [ltok] + hT_toks, btoks)
                    u["b"], u["mm"] = b, t_mm
                    if h2 == 1 and tb == 1:
                        ring_release(i, [t_mm])

                def s1():
                    b = u["b"]
                    t_q = P.op("act", lambda E: E.activation(out=qT, in_=ps[b][:, :], func=AF.Copy), [u["mm"]] + slot_free(n))
                    rel(b, [t_q])
                    u["tq"] = t_q

                def b1_pe():
                    u["sb"] = []
                    for half in range(2):
                        bs_, bstoks = get_bank()
                        P.wait("pe", bstoks)
                        t_s = None
                        for q4 in range(2):
                            t4 = half * 2 + q4
                            fn = (lambda t4=t4, q4=q4, bs_=bs_: (lambda E: E.matmul(ps[bs_][:, q4 * 256:(q4 + 1) * 256], lhsT=qT[:, t4 * 128:(t4 + 1) * 128],
                                                                                  rhs=KT[:, hh, :], start=True, stop=True)))()
                            t_s = P.op("pe", fn, [u["tq"]] + kv_readers, sig=(q4 == 1))
                        u["sb"].append((bs_, t_s))

                def b1_dve():
                    c = new_stat(12)
                    u["c"] = c
                    u["tnm"] = []
                    for half in range(2):
                        bs_, t_s = u["sb"][half]
                        t_mx = P.op("dve", lambda E, bs_=bs_, half=half: E.tensor_reduce(out=ss[:, c + 2 * half:c + 2 * half + 2],
                                                                                          in_=ps[bs_][:, :].rearrange("p (a b) -> p a b", a=2), axis=AX.X, op=ALU.max), [t_s])
                        t_nm = P.op("dve", lambda E, half=half: E.tensor_scalar(out=ss[:, c + 2 * half:c + 2 * half + 2], in0=ss[:, c + 2 * half:c + 2 * half + 2],
                                                                                 scalar1=-ATT_SCALE, scalar2=None, op0=ALU.mult), [t_mx])
                        u["tnm"].append(t_nm)

                def b1_act():
                    sfree = slot_free(n)
                    c = u["c"]
                    te = []
                    for half in range(2):
                        bs_, t_s = u["sb"][half]
                        for q4 in range(2):
                            t4 = half * 2 + q4
                            tl = P.op("act", lambda E, bs_=bs_, q4=q4, t4=t4, half=half: E.activation(out=eT[half][:, q4 * 256:(q4 + 1) * 256], in_=ps[bs_][:, q4 * 256:(q4 + 1) * 256],
                                                                                                    func=AF.Exp, scale=ATT_SCALE, bias=ss[:, c + t4:c + t4 + 1],
                                                                                                    accum_out=ss[:, c + 4 + t4:c + 5 + t4]), [u["tnm"][half], t_s] + sfree)
                            te.append(tl)
                        rel(bs_, te[-2:])
                    u["te"] = te

                def b2a():
                    c = u["c"]
                    t_rz = P.op("dve", lambda E: E.reciprocal(out=ss[:, c + 8:c + 12], in_=ss[:, c + 4:c + 8]), u["te"])
                    tp = []
                    for t4 in range(4):
                        half, q4 = t4 // 2, t4 % 2
                        tp.append(P.op("dve", lambda E, t4=t4, half=half, q4=q4: E.tensor_scalar(out=p_bf[:, t4 * 256:(t4 + 1) * 256], in0=eT[half][:, q4 * 256:(q4 + 1) * 256],
                                                                                                  scalar1=ss[:, c + 8 + t4:c + 9 + t4], scalar2=None, op0=ALU.mult), [t_rz] + u["te"] + slot_free(n)))
                    u["tp"] = tp

                def b2b_pe():
                    tp = u["tp"]
                    bt, bttoks = get_bank()
                    ptv = ps[bt][:, :].bitcast(BF16)
                    P.wait("pe", bttoks)
                    t_tr = None
                    for t4 in range(4):
                        for mc in range(2):
                            k8 = t4 * 2 + mc
                            fn = (lambda t4=t4, mc=mc, k8=k8: (lambda E: E.transpose(ptv[:, k8 * 128:(k8 + 1) * 128], p_bf[:, t4 * 256 + mc * 128:t4 * 256 + (mc + 1) * 128], ident[:, :])))()
                            t_tr = P.op("pe", fn, tp + setup_toks, sig=(k8 == 7))
                    u["bt"], u["ptv"], u["ttr"] = bt, ptv, t_tr

                def b2b_act():
                    t_pt = P.op("dve", lambda E: E.tensor_copy(out=pT, in_=u["ptv"]), [u["ttr"]] + u["tp"])
                    rel(u["bt"], [t_pt])
                    u["tpt"] = t_pt

                def b3_pe():
                    bo, botoks = get_bank()
                    P.wait("pe", botoks)
                    t_pv = None
                    for t4 in range(4):
                        for mc in range(2):
                            k8 = t4 * 2 + mc
                            fn = (lambda t4=t4, mc=mc, k8=k8: (lambda E: E.matmul(ps[bo][:, t4 * 128:(t4 + 1) * 128], lhsT=Vm[:, mc, hh * 128:(hh + 1) * 128],
                                                                                rhs=pT[:, k8 * 128:(k8 + 1) * 128], start=(mc == 0), stop=(mc == 1))))()
                            t_pv = P.op("pe", fn, [u["tpt"]] + kv_readers, sig=(k8 == 7))
                    u["bo"], u["tpv"] = bo, t_pv

                def b3_act():
                    bo = u["bo"]
                    t_o = P.op("act", lambda E: E.activation(out=T2, in_=ps[bo][:, :], func=AF.Copy), [u["tpv"], u["tpt"]])
                    rel(bo, [t_o])
                    u["ta"] = t_o
                    u["tsq"] = P.op("pool", lambda E: E.tensor_tensor(out=T0[:, 0:256].bitcast(BF16), in0=T2, in1=T2, op=ALU.mult), [t_o])

                return [(0, 6, s0), (0, 8, s1), (1, 0, b1_pe), (1, 2, b1_dve), (1, 5, b1_act), (2, 2, b2a),
                        (3, 0, b2b_pe), (3, 3, b2b_act), (4, 0, b3_pe), (4, 3, b3_act)] + c_stages(u, n, T2, 12 + hh, tb, T0, T3, 5)

            def make_kv_units():
                Ms = [R1[:, 11264:13312], headsT[:, 0:4, :].bitcast(F32).rearrange("p a b -> p (a b)")]
                memT = headsT[:, 4:8, :].rearrange("p a (b c) -> p (a b) c", c=256)
                m_slots = [xs_slots[2], xs_slots[3]]
                st8 = kv_state
                memT_toks = []

                def mk(mt):
                    u = {}

                    def s_dma():
                        if mt == 0:
                            st8["tG"] = P.dma("sp", Gt[:, :], g_mem_d.partition_broadcast(128), g_slot, g_last_use)
                        u["tx"] = P.dma("sp", Ms[mt], mem_d[mt * 128:(mt + 1) * 128, :], m_slots[mt], all_xs_free)
                    s_act, s_dve = norm_unit(u, Ms[mt], Gt[:, :], mt, lambda: [u["tx"], st8["tG"]] + setup_toks, st8.setdefault("g_use", []))

                    def s_pe():
                        pe_toks, ev = transpose16(hb[mt], lambda half: memT[:, half * 8:(half + 1) * 8, mt * 128:(mt + 1) * 128],
                                                  [u["th"]] + setup_toks, defer=True)
                        hb_free[mt] = list(pe_toks)
                        u["ev"] = ev

                    def s_ev():
                        memT_toks.extend(u["ev"]())
                        if mt == 1:
                            emit_kv()
                    return [(0, 9, s_dma), (1, 3, s_act), (2, 2, s_dve), (3, 0, s_pe), (4, 8, s_ev)]

                def emit_kv():
                    last_all = []
                    for j in range(4):
                        i, slot_t, ltok = ring_take(f"kv{j}")
                        w = wcols(slot_t, KC, 256)
                        last_pe = []
                        if j < 2:
                            for hh2 in range(2):
                                hh = j * 2 + hh2
                                b, btoks = get_bank()
                                t_mm = mm_group(ps[b][:, 0:256], [(w[:, kc, hh2 * 128:(hh2 + 1) * 128], memT[:, kc, :]) for kc in range(KC)],
                                                [ltok] + memT_toks, btoks)
                                t_c = P.op("act", lambda E, b=b, hh=hh: E.activation(out=KT[:, hh, :], in_=ps[b][:, 0:256], func=AF.Copy), [t_mm])
                                rel(b, [t_c])
                                last_pe.append(t_mm)
                                kv_readers.append(t_c)
                        else:
                            for mc in range(2):
                                b, btoks = get_bank()
                                t_mm = mm_group(ps[b][:, 0:256], [(memT[:, kc, mc * 128:(mc + 1) * 128], w[:, kc, :]) for kc in range(KC)],
                                                [ltok] + memT_toks, btoks)
                                t_c = P.op("act", lambda E, b=b, mc=mc, j=j: E.activation(out=Vm[:, mc, (j - 2) * 256:(j - 1) * 256], in_=ps[b][:, 0:256], func=AF.Copy), [t_mm])
                                rel(b, [t_c])
                                last_pe.append(t_mm)
                                kv_readers.append(t_c)
                        ring_release(i, last_pe)
                        last_all = last_pe
                    heads_free.extend(kv_readers)
                    uslot_free[0] = list(kv_readers) + list(all_xs_free)
                return [mk(0), mk(1)]

            kv_state = {}
            kv_units = make_kv_units() if st == 0 else [[], []]
            units = [make_v_unit(tt) for tt in range(TPS)] + kv_units + [[], [], [], []]
            for hh in range(4):
                for tb in range(2):
                    units.append(make_a_unit(len(units), hh, tb))
            for hd in range(6):
                for tb in range(2):
                    units.append(make_u_unit(len(units), hd, tb))
            for g in range(6):
                for tb in range(2):
                    units.append(make_c_unit(len(units), g, tb))
            run_sched(units)
            g_last_use = list(g_last_use) + list(kv_state.get("g_use", []))
            vn_all = [vn_tok[tt] for tt in range(TPS)]
            heads_toks = list(ph2_toks)
            heads_free = []
            ph2_end = [t for sl_ in uslot_free if sl_ for t in sl_] + vn_all[-1:]

            xr_toks = []
            for tt in range(TPS):
                xr_toks.append(P.dma("sp", xres[:, tt, :], x_d[tok0 + tt * 128: tok0 + (tt + 1) * 128, :], xr_slots[tt],
                                     heads_toks[-1:] + ph2_end + hT_toks[-2:] + list(prev_st_done)))
            tGf = P.dma("sp", Gt[:, :], g_ffn_d.partition_broadcast(128), g_slot, g_last_use)
            g_last_use = []
            x1_toks = [[] for _ in range(TPS)]
            wo_last = []

            def wo_group(j, tt, w, ltok):
                b, btoks = get_bank()
                t_mm = mm_group(ps[b][:, 0:256], [(headsT[:, kc, tt * 128:(tt + 1) * 128], w[:, kc, :]) for kc in range(KC)],
                                [ltok] + heads_toks, btoks)
                dst = xres[:, tt, j * 256:(j + 1) * 256]
                t_add = P.op("dve", lambda E: E.tensor_tensor(out=dst, in0=dst, in1=ps[b][:, 0:256], op=ALU.add), [t_mm, xr_toks[tt]])
                rel(b, [t_add])
                x1_toks[tt].append(t_add)
                return t_mm

            for j in range(7):
                i, slot_t, ltok = ring_take(f"wo{st}_{j}")
                w = wcols(slot_t, KC, 256)
                for tt in range(TPS):
                    wo_last = [wo_group(j, tt, w, ltok)]
                ring_release(i, wo_last)
            i7, slot7, ltok7 = ring_take(f"wo{st}_7")
            w7 = wcols(slot7, KC, 256)
            n2_toks = []
            g_use = []
            wo_fin = {}

            def make_p3_unit(tt):
                u = {}
                hsl = tt % 2

                def s_mm():
                    wo_fin[tt] = wo_group(7, tt, w7, ltok7)
                    if tt == TPS - 1:
                        ring_release(i7, [wo_fin[tt]])
                s_act, s_dve = norm_unit(u, xres[:, tt, :], Gt[:, :], hsl, lambda: x1_toks[tt][-1:] + [tGf], g_use)

                def s_pe():
                    pe_toks, ev = transpose16(hb[hsl], lambda half: actT[:, half * 8:(half + 1) * 8, tt * 128:(tt + 1) * 128],
                                              [u["th"]] + heads_toks[-1:] + ph2_end, defer=True)
                    hb_free[hsl] = list(pe_toks)
                    u["ev"] = ev

                def s_ev():
                    n2_toks.extend(u["ev"]())
                return [(0, 0, s_mm), (1, 1, s_act), (2, 1, s_dve), (3, 1, s_pe), (4, 1, s_ev)]

            run_sched([make_p3_unit(tt) for tt in range(TPS)])
            wo_last = [wo_fin[TPS - 1]]
            g_last_use = g_use

            nfc = DFF // 512
            g_ready = [[], []]
            g_readers = [list(wo_last), list(wo_last)]
            x2_last = [x1_toks[tt][-1:] for tt in range(TPS)]
            rs_free = [[], []]
            rs_i = [0]

            def ffn1(fc):
                buf = fc % 2
                toks = []
                for p_ in range(2):
                    i, slot_t, ltok = ring_take(f"w1_{st}_{fc}_{p_}")
                    w = wcols(slot_t, KC, 256)
                    last_pe = []
                    for fb in range(2):
                        for tb in range(2):
                            b, btoks = get_bank()
                            t_mm = mm_group(ps[b][:, :], [(w[:, kc, fb * 128:(fb + 1) * 128], actT[:, kc, tb * 512:(tb + 1) * 512]) for kc in range(KC)],
                                            [ltok] + n2_toks, btoks)
                            last_pe = [t_mm]
                            k = rs_i[0] % 2
                            rs_i[0] += 1
                            t_r = P.op("act", lambda E, b=b, k=k: E.activation(out=rsc[k], in_=ps[b][:, :], func=AF.Relu), [t_mm] + rs_free[k])
                            rel(b, [t_r])
                            dst = gT[buf][:, p_ * 2 + fb, tb * 512:(tb + 1) * 512]
                            t_g = P.op("pool", lambda E, k=k, dst=dst: E.tensor_tensor(out=dst, in0=rsc[k], in1=rsc[k], op=ALU.mult), [t_r] + g_readers[buf])
                            rs_free[k] = [t_g]
                            toks.append(t_g)
                    ring_release(i, last_pe)
                g_readers[buf] = []
                g_ready[buf] = toks

            def ffn2(fc, tile_hook=None):
                buf = fc % 2
                pcs = [ring_take(f"w2_{st}_{fc}_{p_}") for p_ in range(2)]
                ws = [wcols(pc[1], 2, D) for pc in pcs]
                last_pe = []
                for tt in range(TPS):
                    if tile_hook is not None:
                        tile_hook(tt)
                    for cg in range(4):
                        b, btoks = get_bank()
                        pairs = [(gT[buf][:, fbk, tt * 128:(tt + 1) * 128], ws[fbk // 2][:, fbk % 2, cg * 512:(cg + 1) * 512]) for fbk in range(4)]
                        t_mm = mm_group(ps[b][:, :], pairs, [pcs[0][2], pcs[1][2]] + g_ready[buf], btoks)
                        last_pe = [t_mm]
                        dst = xres[:, tt, cg * 512:(cg + 1) * 512]
                        t_add = P.op("dve", lambda E, b=b, dst=dst: E.tensor_tensor(out=dst, in0=dst, in1=ps[b][:, :], op=ALU.add), [t_mm])
                        rel(b, [t_add])
                        x2_last[tt] = [t_add]
                g_readers[buf] = list(last_pe)
                for pc in pcs:
                    ring_release(pc[0], last_pe)
                return last_pe

            tGo = P.dma("sp", Gt[:, :], g_fin_d.partition_broadcast(128), g_slot, g_last_use)
            g_last_use = []
            g_use = []
            st_store = {}

            p5 = {}

            def p5_act(tt):
                u = p5.setdefault(tt, {})
                src = xres[:, tt, :]
                c = new_stat(2)
                u["c"] = c
                t0 = P.op("act", lambda E: E.activation(out=junk[:, :], in_=src, func=AF.Square, accum_out=ss[:, c:c + 1]), x2_last[tt] + junk_tok[0])
                junk_tok[0] = [t0]
                u["t0"] = t0
                u["t1"] = rstd_from_sum(c, c + 1, 1.0 / D, [t0])

            def p5_dve(tt):
                u = p5[tt]
                src = xres[:, tt, :]
                c = u["c"]
                t2 = P.op("dve", lambda E: E.scalar_tensor_tensor(out=src, in0=src, scalar=ss[:, c + 1:c + 2], in1=Gt[:, :],
                                                                   op0=ALU.mult, op1=ALU.mult), [u["t1"], u["t0"], tGo])
                g_use.append(t2)
                u["t2"] = t2

            def p5_out(tt):
                src = xres[:, tt, :]
                t3 = P.dma("sp", out_d[tok0 + tt * 128: tok0 + (tt + 1) * 128, :], src, o_slots[tt], [p5[tt]["t2"]])
                out_toks.append(t3)
                st_store[tt] = t3

            def p5_hook(tt):
                if tt - 1 >= 0:
                    p5_act(tt - 1)
                if tt - 2 >= 0:
                    p5_dve(tt - 2)
                if tt - 3 >= 0:
                    p5_out(tt - 3)

            ffn1(0)
            f2_last = []
            for fc in range(nfc):
                if fc + 1 < nfc:
                    ffn1(fc + 1)
                f2_last = ffn2(fc, p5_hook if fc == nfc - 1 else None)
            heads_free = list(f2_last)
            actT_free = list(f2_last)
            p5_act(TPS - 1)
            p5_dve(TPS - 2)
            p5_out(TPS - 3)
            p5_dve(TPS - 1)
            p5_out(TPS - 2)
            p5_out(TPS - 1)
            g_last_use = g_use
            prev_st_done = [st_store[tt] for tt in range(TPS)]
            prev_store = dict(st_store)
            r1_free = []

        P.wait("sp", out_toks)
        assert ring_state["taken"] == len(pieces), (ring_state, len(pieces))

        with nc.Block() as block:
            @block.tensor
            def _(E):
                for f in P.q["pe"]:
                    f(E)

            @block.scalar
            def _(E):
                for f in P.q["act"]:
                    f(E)

            @block.vector
            def _(E):
                for f in P.q["dve"]:
                    f(E)

            @block.gpsimd
            def _(E):
                for f in P.q["pool"]:
                    f(E)

            @block.sync
            def _(E):
                for f in P.q["sp"]:
                    f(E)
    return nc


_NC_CACHE = {}


def kernel(x, mem, g_mix, w_in, ln_v_g, ln_v_b, w_s, b_s, conv_w, g_mem, w_kv, g_head, w_o, g_ffn,
           w_ffn1, w_ffn2, g_final):
    f = lambda a: np.ascontiguousarray(np.asarray(a, dtype=np.float32))
    x = f(x)
    mem = f(mem)
    shared = {
        "g_mix": f(g_mix).reshape(D),
        "w_in": f(w_in).reshape(D, D_IN),
        "ln_v_g": f(ln_v_g).reshape(768),
        "ln_v_b": f(ln_v_b).reshape(768),
        "w_sT": f(np.asarray(w_s).reshape(6, 128, 128).transpose(0, 2, 1)),
        "b_s": f(b_s).reshape(768),
        "conv_wp": f(np.asarray(conv_w).reshape(3, 6, 128).transpose(2, 1, 0).reshape(128, 18)),
        "g_mem": f(g_mem).reshape(D),
        "w_kv": f(w_kv).reshape(D, 1024),
        "g_headp": f(np.asarray(g_head).reshape(16, 128).T),
        "w_o": f(w_o).reshape(D, D),
        "g_ffn": f(g_ffn).reshape(D),
        "w_ffn1": f(w_ffn1).reshape(D, DFF),
        "w_ffn2": f(w_ffn2).reshape(DFF, D),
        "g_final": f(g_final).reshape(D),
    }
    if "nc" not in _NC_CACHE:
        _NC_CACHE["nc"] = build_nc()
    nc = _NC_CACHE["nc"]
    in_maps = []
    for b in range(8):
        m = dict(shared)
        m["x"] = x[b]
        m["mem"] = mem[b]
        in_maps.append(m)
    res = run_bass_kernel_spmd(nc, in_maps, core_ids=list(range(8)))
    out = np.stack([np.asarray(r["out"], dtype=np.float32) for r in res.results], axis=0)
    return out
```

```python
from contextlib import ExitStack

import numpy as np
import concourse.bass as bass
import concourse.mybir as mybir
from concourse.bass_utils import run_bass_kernel_spmd

F32 = mybir.dt.float32
BF16 = mybir.dt.bfloat16
AF = mybir.ActivationFunctionType
ALU = mybir.AluOpType
AX = mybir.AxisListType

D = 2048
SEQ = 2048
NMEM = 256
D_IN = 4352
DFF = 8192
KC = 16
TS = 1024
NST = SEQ // TS
TPS = TS // 128
NSLOT = 6
EPS = 1e-6
C_U, C_V, C_B, C_C, C_X, C_Q = 0, 768, 1536, 2304, 3072, 3840
GELU_C2 = 1.5957691216057308
ATT_SCALE = 128 ** -0.5

DEBUG = False


class Tok:
    __slots__ = ("sem", "val")

    def __init__(self, sem, val):
        self.sem = sem
        self.val = val


class Slot:
    def __init__(self, sem):
        self.sem = sem
        self.cnt = 0


class Prog:
    ENGS = ("pe", "act", "dve", "pool", "sp")

    def __init__(self, psem):
        self.q = {e: [] for e in self.ENGS}
        self.psem = psem
        self.cnt = {e: 0 for e in self.ENGS}
        self.waited = {e: {} for e in self.ENGS}

    def wait(self, eng, tok):
        if tok is None:
            return
        if isinstance(tok, (list, tuple)):
            for t in tok:
                self.wait(eng, t)
            return
        key = id(tok.sem)
        if self.waited[eng].get(key, 0) >= tok.val:
            return
        self.waited[eng][key] = tok.val
        sem, val = tok.sem, tok.val
        self.q[eng].append(lambda E: E.wait_ge(sem, val))

    def op(self, eng, fn, deps=(), sig=True):
        self.wait(eng, deps)
        if sig:
            self.cnt[eng] += 1
            val = self.cnt[eng]
            sem = self.psem[eng]
            self.q[eng].append(lambda E: fn(E).then_inc(sem, 1))
            return Tok(sem, val)
        self.q[eng].append(fn)
        return None

    def dma(self, eng, out, in_, slot, deps=()):
        self.wait(eng, deps)
        slot.cnt += 16
        val = slot.cnt
        sem = slot.sem
        self.q[eng].append(lambda E: E.dma_start(out=out, in_=in_).then_inc(sem, 16))
        return Tok(sem, val)


def build_nc():
    nc = bass.Bass("TRN2", target_bir_lowering=False)
    dt = lambda name, shape, kind="ExternalInput": nc.dram_tensor(name, shape, F32, kind=kind).ap()
    x_d = dt("x", [SEQ, D])
    mem_d = dt("mem", [NMEM, D])
    g_mix_d = dt("g_mix", [D])
    w_in_d = dt("w_in", [D, D_IN])
    lng_d = dt("ln_v_g", [768])
    lnb_d = dt("ln_v_b", [768])
    wst_d = dt("w_sT", [6, 128, 128])
    bs_d = dt("b_s", [768])
    cw_d = dt("conv_wp", [128, 18])
    g_mem_d = dt("g_mem", [D])
    w_kv_d = dt("w_kv", [D, 1024])
    gh_d = dt("g_headp", [128, 16])
    w_o_d = dt("w_o", [D, D])
    g_ffn_d = dt("g_ffn", [D])
    w1_d = dt("w_ffn1", [D, DFF])
    w2_d = dt("w_ffn2", [DFF, D])
    g_fin_d = dt("g_final", [D])
    out_d = dt("out", [SEQ, D], kind="ExternalOutput")
    if DEBUG:
        dbg_d = dt("dbg", [128, 16 * 1024], kind="ExternalOutput")

    with ExitStack() as es:
        sb = lambda name, shape, dty: es.enter_context(nc.sbuf_tensor(name, shape, dty))
        R1 = sb("R1", [128, 16384], F32)
        actT = sb("actT", [128, KC, TS], BF16)
        headsT = sb("headsT", [128, KC, TS], BF16)
        ring = [sb(f"ring{i}", [128, 4096], BF16) for i in range(NSLOT)]
        Gt = sb("G", [128, D], F32)
        biasT = sb("biasT", [128, 6, 128], F32)
        WsT = sb("WsT", [128, 6, 128], BF16)
        KT = sb("KT", [128, 4, 256], BF16)
        Vm = sb("Vm", [128, 2, 512], BF16)
        ident = sb("ident", [128, 128], BF16)
        onesM = sb("onesM", [128, 128], BF16)
        cw = sb("cw", [128, 18], F32)
        gh = sb("gh", [128, 16], F32)
        hist = sb("hist", [128, 12], F32)
        ss = sb("ss", [128, 512], F32)
        hb = [sb(f"hb{i}", [128, D], BF16) for i in range(2)]
        junk = sb("junk", [128, D], BF16)
        hb_free = [[], []]
        junk_tok = [[]]
        ps = [es.enter_context(nc.psum_tensor(f"ps{i}", [128, 512], F32)) for i in range(8)]

        sem = lambda name: es.enter_context(nc.semaphore(name))
        psem = {e: sem(f"p_{e}") for e in Prog.ENGS}
        P = Prog(psem)
        ring_slots = [Slot(sem(f"ring{i}")) for i in range(NSLOT)]
        xs_slots = [Slot(sem(f"xs{i}")) for i in range(4)]
        xr_slots = [Slot(sem(f"xr{i}")) for i in range(TPS)]
        g_slot = Slot(sem("g"))
        ln_slot = Slot(sem("ln"))
        c_slot = Slot(sem("consts"))
        o_slots = [Slot(sem(f"outst{i}")) for i in range(TPS)]

        xres = R1[:, :].rearrange("p (t d) -> p t d", t=TPS)
        XS_OFF = [0, 2048, 4096, 6144]
        xs = [R1[:, o_:o_ + 2048] for o_ in XS_OFF]
        v_n = R1[:, 4096:4096 + 3072].bitcast(BF16).rearrange("p (t d) -> p t d", t=TPS)
        SC0 = 4096 + 3072
        scr = [R1[:, SC0 + i * 512: SC0 + (i + 1) * 512] for i in range(8)]
        VG0 = SC0 + 8 * 512
        vg = [R1[:, VG0: VG0 + 768], R1[:, VG0 + 768: VG0 + 1536]]
        XC0 = VG0 + 1536
        xcb = R1[:, XC0: XC0 + 516]
        LNG = R1[:, 13440:13440 + 768]
        LNB = R1[:, 14208:14208 + 768]
        gT = [headsT[:, 0:4, :], headsT[:, 4:8, :]]
        rsc = [headsT[:, 8 + i, :].bitcast(F32) for i in range(2)]

        bank_free = [[] for _ in range(8)]
        bank_rr = [0]

        held = set()

        def get_bank():
            for _ in range(8):
                b = bank_rr[0]
                bank_rr[0] = (b + 1) % 8
                if b not in held:
                    break
            else:
                raise RuntimeError("all PSUM banks held")
            held.add(b)
            toks = bank_free[b]
            bank_free[b] = []
            return b, toks

        def rel(b, toks):
            bank_free[b] += list(toks)
            held.discard(b)

        pieces = []

        def plan():
            def win(name, c0, w):
                pieces.append((name, [(0, w_in_d[:, c0:c0 + w].rearrange("(kc p) c -> p kc c", p=128), KC, w)]))
            for st in range(NST):
                for j in range(3):
                    win(f"v{st}_{j}", C_V + j * 256, 256)
                if st == 0:
                    for j in range(4):
                        pieces.append((f"kv{j}", [(0, w_kv_d[:, j * 256:(j + 1) * 256].rearrange("(kc p) c -> p kc c", p=128), KC, 256)]))
                for j in range(2):
                    win(f"q{st}_{j}", C_Q + j * 256, 256)
                for j in range(3):
                    win(f"u{st}_{j}", C_U + j * 256, 256)
                for gp in range(3):
                    for k, c0 in enumerate((C_B, C_C, C_X)):
                        win(f"cv{st}_{gp}_{k}", c0 + gp * 256, 256)
                for j in range(8):
                    pieces.append((f"wo{st}_{j}", [(0, w_o_d[:, j * 256:(j + 1) * 256].rearrange("(kc p) c -> p kc c", p=128), KC, 256)]))
                nfc = DFF // 512
                def f1(fc):
                    for p_ in range(2):
                        c0 = fc * 512 + p_ * 256
                        pieces.append((f"w1_{st}_{fc}_{p_}", [(0, w1_d[:, c0:c0 + 256].rearrange("(kc p) c -> p kc c", p=128), KC, 256)]))
                def f2(fc):
                    for p_ in range(2):
                        r0 = fc * 512 + p_ * 256
                        pieces.append((f"w2_{st}_{fc}_{p_}", [(0, w2_d[r0:r0 + 256, :].rearrange("(fb p) c -> p fb c", p=128), 2, D)]))
                f1(0)
                for fc in range(nfc):
                    if fc + 1 < nfc:
                        f1(fc + 1)
                    f2(fc)

        plan()
        ring_state = {"issued": 0, "taken": 0}
        slot_release = [[] for _ in range(NSLOT)]
        piece_tok = {}

        def ring_issue():
            while ring_state["issued"] < len(pieces) and ring_state["issued"] < ring_state["released"] + NSLOT:
                i = ring_state["issued"]
                name, segs = pieces[i]
                s = i % NSLOT
                P.wait("pool", slot_release[s])
                slot_release[s] = []
                tok = None
                for (off, src, a, b) in segs:
                    dst = ring[s][:, off:off + a * b].rearrange("p (a b) -> p a b", a=a)
                    tok = P.dma("pool", dst, src, ring_slots[s])
                piece_tok[i] = tok
                ring_state["issued"] += 1

        ring_state["released"] = 0

        def ring_take(name):
            i = ring_state["taken"]
            assert pieces[i][0] == name, (pieces[i][0], name)
            assert i < ring_state["issued"], "piece not issued (lookahead too small)"
            ring_state["taken"] += 1
            return i, ring[i % NSLOT], piece_tok[i]

        def ring_release(i, toks):
            slot_release[i % NSLOT] = list(toks)
            ring_state["released"] += 1
            ring_issue()

        stat_col = [0]

        def new_stat(n=1):
            c = stat_col[0]
            stat_col[0] += n
            assert stat_col[0] <= 512
            return c

        def stat_reset(deps):
            stat_col[0] = 0

        def rstd_from_sum(col_in, col_out, inv_n, deps):
            t1 = P.op("act", lambda E: E.activation(out=ss[:, col_out:col_out + 1], in_=ss[:, col_in:col_in + 1],
                                                     func=AF.Ln, bias=EPS, scale=inv_n), deps)
            t2 = P.op("act", lambda E: E.activation(out=ss[:, col_out:col_out + 1], in_=ss[:, col_out:col_out + 1],
                                                     func=AF.Exp, scale=-0.5), [t1])
            return t2

        def mm_group(out_ap, pairs, deps, bank_toks):
            P.wait("pe", deps)
            P.wait("pe", bank_toks)
            n = len(pairs)
            tok = None
            for i, (l, r) in enumerate(pairs):
                last = i == n - 1
                fn = (lambda l=l, r=r, st=(i == 0), sp_=last: (lambda E: E.matmul(out_ap, lhsT=l, rhs=r, start=st, stop=sp_)))()
                tok = P.op("pe", fn, (), sig=last)
            return tok

        def transpose16(src_bf, dst_fn, deps, evac_engs=("act", "dve"), defer=False):
            pend = []
            for half in range(2):
                b, btoks = get_bank()
                pview = ps[b][:, :].bitcast(BF16)
                P.wait("pe", deps)
                P.wait("pe", btoks)
                t = None
                for j in range(8):
                    kc = half * 8 + j
                    fn = (lambda kc=kc, j=j, pview=pview: (lambda E: E.transpose(pview[:, j * 128:(j + 1) * 128], src_bf[:, kc * 128:(kc + 1) * 128], ident[:, :])))()
                    t = P.op("pe", fn, (), sig=(j == 7))
                pend.append((half, b, pview, t))

            def evac():
                toks = []
                for (half, b, pview, t) in pend:
                    eng = evac_engs[half % len(evac_engs)]
                    dst = dst_fn(half)
                    src = pview.rearrange("p (a b) -> p a b", a=8)
                    if eng == "act":
                        te = P.op("act", lambda E, dst=dst, src=src: E.activation(out=dst, in_=src, func=AF.Copy), [t])
                    else:
                        te = P.op("dve", lambda E, dst=dst, src=src: E.tensor_copy(out=dst, in_=src), [t])
                    rel(b, [te])
                    toks.append(te)
                return toks
            if defer:
                return [p_[3] for p_ in pend], evac
            return evac()

        def rmsnorm_tile(src, G, hbuf, deps, hbuf_free):
            c = new_stat(2)
            t0 = P.op("act", lambda E: E.activation(out=hbuf[:, :], in_=src, func=AF.Square, accum_out=ss[:, c:c + 1]),
                      list(deps) + list(hbuf_free))
            t1 = rstd_from_sum(c, c + 1, 1.0 / D, [t0])
            t2 = P.op("dve", lambda E: E.scalar_tensor_tensor(out=hbuf[:, :], in0=src, scalar=ss[:, c + 1:c + 2], in1=G,
                                                               op0=ALU.mult, op1=ALU.mult), [t1, t0] + list(deps))
            return t2

        def gelu(src_ps, xs_t, tmp_a, tmp_b, out_t, deps, n):
            t_x = P.op("act", lambda E: E.activation(out=xs_t, in_=src_ps, func=AF.Copy), deps)
            t_s = P.op("act", lambda E: E.activation(out=tmp_a, in_=src_ps, func=AF.Square), deps)
            t_w = P.op("dve", lambda E: E.tensor_scalar(out=tmp_a, in0=tmp_a, scalar1=0.044715 * GELU_C2, scalar2=GELU_C2,
                                                         op0=ALU.mult, op1=ALU.add), [t_s])
            t_z = P.op("dve", lambda E: E.tensor_tensor(out=tmp_a, in0=tmp_a, in1=xs_t, op=ALU.mult), [t_w, t_x])
            t_e = P.op("act", lambda E: E.activation(out=tmp_b, in_=tmp_a, func=AF.Exp, scale=-1.0), [t_z])
            t_d = P.op("dve", lambda E: E.tensor_scalar(out=tmp_b, in0=tmp_b, scalar1=1.0, scalar2=None, op0=ALU.add), [t_e])
            t_r = P.op("dve", lambda E: E.reciprocal(out=tmp_b, in_=tmp_b), [t_d])
            t_g = P.op("dve", lambda E: E.tensor_tensor(out=out_t, in0=xs_t, in1=tmp_b, op=ALU.mult), [t_r, t_x])
            return t_g, [t_x, t_s]

        def head_rms(a_t, sq_bf, rs_t, blk, tb, deps):
            t_sq = P.op("act", lambda E: E.activation(out=sq_bf, in_=a_t, func=AF.Square), deps)
            b, btoks = get_bank()
            t_mm = mm_group(ps[b][:, :], [(onesM[:, :], sq_bf)], [t_sq], btoks)
            t_l = P.op("act", lambda E: E.activation(out=rs_t, in_=ps[b][:, :], func=AF.Ln, bias=EPS), [t_mm])
            rel(b, [t_l])
            t_r = P.op("act", lambda E: E.activation(out=rs_t, in_=rs_t, func=AF.Exp, scale=-0.5), [t_l])
            dst = headsT[:, blk, tb * 512:(tb + 1) * 512]
            t_h = P.op("dve", lambda E: E.scalar_tensor_tensor(out=dst, in0=a_t, scalar=gh[:, blk:blk + 1], in1=rs_t,
                                                                op0=ALU.mult, op1=ALU.mult), [t_r] + list(deps))
            return t_h

        def wcols(slot_t, a, b):
            return slot_t[:, 0:a * b].rearrange("p (a b) -> p a b", a=a)

        def run_sched(units):
            items = []
            for n, stages in enumerate(units):
                for si, (off, cls, fn) in enumerate(stages):
                    items.append((n + off, cls, n, si, fn))
            items.sort(key=lambda it: it[:4])
            for it in items:
                it[4]()

        def norm_unit(u, src, G, hslot, deps_fn, g_use, gdeps_fn=None):
            def s_act():
                c = new_stat(2)
                u["c"] = c
                t0 = P.op("act", lambda E: E.activation(out=junk[:, :], in_=src, func=AF.Square, accum_out=ss[:, c:c + 1]), deps_fn() + junk_tok[0])
                junk_tok[0] = [t0]
                u["t0"] = t0
                u["t1"] = rstd_from_sum(c, c + 1, 1.0 / D, [t0])

            def s_dve():
                c = u["c"]
                t2 = P.op("dve", lambda E: E.scalar_tensor_tensor(out=hb[hslot][:, :], in0=src, scalar=ss[:, c + 1:c + 2], in1=G,
                                                                   op0=ALU.mult, op1=ALU.mult), [u["t1"], u["t0"]] + deps_fn() + (gdeps_fn() if gdeps_fn else []) + hb_free[hslot])
                u["th"] = t2
                g_use.append(t2)
            return s_act, s_dve

        c_toks = []
        c_toks.append(P.dma("sp", biasT[:, :, :], bs_d.partition_broadcast(128).rearrange("p (h t) -> p h t", h=6), c_slot))
        wst_stage = R1[:, 14976:14976 + 768].rearrange("p (h t) -> p h t", h=6)
        c_toks.append(P.dma("sp", wst_stage, wst_d.rearrange("h s t -> s h t"), c_slot))
        c_toks.append(P.dma("sp", cw[:, :], cw_d, c_slot))
        c_toks.append(P.dma("sp", gh[:, :], gh_d, c_slot))
        c_all = c_toks[-1]

        id_stage = R1[:, 15744:15744 + 128]
        t_i0 = P.op("pool", lambda E: E.memset(id_stage, 0.0))
        t_i1 = P.op("pool", lambda E: E.affine_select(out=id_stage, in_=id_stage, pattern=[[-1, 128]], compare_op=ALU.not_equal,
                                                       fill=1.0, base=0, channel_multiplier=1), [t_i0])
        t_id = P.op("pool", lambda E: E.tensor_copy(out=ident[:, :], in_=id_stage), [t_i1])
        t_on = P.op("pool", lambda E: E.memset(onesM[:, :], 1.0 / 128))
        t_h0 = P.op("pool", lambda E: E.memset(hist[:, :], 0.0))
        t_s0 = P.op("pool", lambda E: E.memset(ss[:, :], 0.0))
        t_w0 = P.op("pool", lambda E: E.affine_select(out=wst_stage, in_=wst_stage, pattern=[[0, 6], [1, 128]], compare_op=ALU.is_ge,
                                                       fill=0.0, base=0, channel_multiplier=-1), [c_all])
        t_ws = P.op("pool", lambda E: E.tensor_copy(out=WsT[:, :, :], in_=wst_stage), [t_w0])
        setup_toks = [t_id, t_on, t_h0, t_s0, t_ws, c_all]
        for e_ in ("pe", "act", "dve"):
            P.wait(e_, setup_toks)

        ring_issue()

        g_last_use = []
        kv_readers = []
        actT_free = []

        r1_free = []
        out_toks = []
        prev_st_done = []
        prev_store = {}
        heads_free = []
        for st in range(NST):
            tok0 = st * TS
            stat_col[0] = 0
            p1s = {}
            xs_free = [[], [], [], []]
            hT_toks = []
            g_use = []
            p1_units = []
            hT_by_tile = {}

            def make_p1_unit(tt):
                u = {}
                xsl = tt % 4
                hsl = tt % 2

                def s_dma():
                    lo_, hi_ = XS_OFF[xsl], XS_OFF[xsl] + 2048
                    ov = [t_ for t_ in range(TPS) if t_ * 2048 < hi_ and (t_ + 1) * 2048 > lo_]
                    pdeps = [prev_store[o_] for o_ in ov] if (tt < 4 and prev_store) else []
                    u["tx"] = P.dma("sp", xs[xsl], x_d[tok0 + tt * 128: tok0 + (tt + 1) * 128, :], xs_slots[xsl],
                                    xs_free[xsl] + pdeps + (r1_free if tt < 4 else []))
                    if tt == 2:
                        p1s["tGm"] = P.dma("sp", Gt[:, :], g_mix_d.partition_broadcast(128), g_slot, g_last_use)
                s_act, s_dve0 = norm_unit(u, xs[xsl], Gt[:, :], hsl, lambda: [u["tx"]], g_use, gdeps_fn=lambda: [p1s["tGm"]])

                def s_dve():
                    s_dve0()
                    xs_free[xsl] = [u["th"]]

                def s_pe():
                    pe_toks, ev = transpose16(hb[hsl], lambda half: actT[:, half * 8:(half + 1) * 8, tt * 128:(tt + 1) * 128],
                                              [u["th"]] + actT_free, defer=True)
                    hb_free[hsl] = list(pe_toks)
                    u["ev"] = ev

                def s_ev():
                    tts = u["ev"]()
                    hT_toks.extend(tts)
                    hT_by_tile[tt] = tts
                return [(0, 0, s_dma), (1, 1, s_act), (2, 1, s_dve), (3, 1, s_pe), (4, 1, s_ev)]

            for tt in range(TPS):
                p1_units.append(make_p1_unit(tt))
            run_sched(p1_units)
            g_last_use = g_use
            ln_toks = [P.dma("sp", LNG, lng_d.partition_broadcast(128), ln_slot, prev_st_done + r1_free),
                       P.dma("sp", LNB, lnb_d.partition_broadcast(128), ln_slot, prev_st_done + r1_free)]
            ln_tok = ln_toks[-1]
            actT_free = []
            ph2_toks = []

            K1, K2 = 0.044715 * GELU_C2, GELU_C2

            def sigmoid_chain(T, deps):
                t1 = P.op("act", lambda E: E.activation(out=T, in_=T, func=AF.Exp, scale=-1.0), deps)
                t2 = P.op("act", lambda E: E.activation(out=T, in_=T, func=AF.Ln, bias=1.0), [t1])
                t3 = P.op("act", lambda E: E.activation(out=T, in_=T, func=AF.Exp, scale=-1.0), [t2])
                return t3

            all_xs_free = [t for l_ in xs_free for t in l_] + hT_toks[-2:] + list(prev_st_done)
            VB = [0, 768, 1536, 2304, 3072, 7168, 7936, 8704, 9472, 10240]
            vbuf_free = [list(all_xs_free) for _ in VB]
            vb_rr = [0]

            def vbuf():
                k = vb_rr[0] % len(VB)
                vb_rr[0] += 1
                fr = vbuf_free[k]
                vbuf_free[k] = None
                return k, R1[:, VB[k]:VB[k] + 768], fr

            vp = [ring_take(f"v{st}_{j}") for j in range(3)]
            vn_tok = {}

            def make_v_unit(tt):
                u = {}

                def s0():
                    u["banks"] = []
                    for (c0, wd, pcs) in ((0, 512, (0, 1)), (512, 256, (2,))):
                        b, btoks = get_bank()
                        last = None
                        for pj in pcs:
                            w = wcols(vp[pj][1], KC, 256)
                            o_ap = ps[b][:, (pj * 256 - c0):(pj * 256 - c0) + 256]
                            last = mm_group(o_ap, [(actT[:, kc, tt * 128:(tt + 1) * 128], w[:, kc, :]) for kc in range(KC)],
                                            [vp[pj][2]] + hT_toks, btoks if pj == pcs[0] else [])
                        u["banks"].append((b, c0, wd, last))
                    if tt == TPS - 1:
                        for j in range(3):
                            ring_release(vp[j][0], [u["banks"][-1][3]])

                def s1():
                    kx, Tx, fx = vbuf()
                    u["kx"], u["Tx"] = kx, Tx
                    tx = []
                    for (b, c0, wd, last) in u["banks"]:
                        t1 = P.op("act", lambda E, b=b, c0=c0, wd=wd: E.activation(out=Tx[:, c0:c0 + wd], in_=ps[b][:, 0:wd], func=AF.Copy), [last] + fx)
                        rel(b, [t1])
                        tx.append(t1)
                    u["tx"] = tx

                def s1p():
                    ka, Ta, fa = vbuf()
                    u["ka"], u["Ta"] = ka, Ta
                    Tx = u["Tx"]
                    u["ts"] = [P.op("pool", lambda E: E.tensor_tensor(out=Ta, in0=Tx, in1=Tx, op=ALU.mult), u["tx"] + fa)]

                def s2():
                    Tx, Ta = u["Tx"], u["Ta"]
                    t_w = P.op("dve", lambda E: E.tensor_scalar(out=Ta, in0=Ta, scalar1=K1, scalar2=K2, op0=ALU.mult, op1=ALU.add), u["ts"])
                    u["tz"] = P.op("dve", lambda E: E.tensor_tensor(out=Ta, in0=Ta, in1=Tx, op=ALU.mult), [t_w] + u["tx"])

                def s3():
                    u["tr"] = sigmoid_chain(u["Ta"], [u["tz"]])

                def s4():
                    Tx, Ta = u["Tx"], u["Ta"]
                    c = new_stat(6)
                    u["c"] = c
                    u["tg"] = P.op("dve", lambda E: E.scalar_tensor_tensor(out=Ta, in0=Tx, scalar=1.0, in1=Ta, op0=ALU.mult, op1=ALU.mult,
                                                                            accum_out=ss[:, c:c + 1]), [u["tr"]] + u["tx"])
                    u["s1"] = u["tg"]
                    vbuf_free[u["kx"]] = [u["tg"]]

                def s5():
                    Ta = u["Ta"]
                    c = u["c"]
                    jk = junk[:, 0:768]
                    u["s2"] = P.op("act", lambda E: E.activation(out=jk, in_=Ta, func=AF.Square, accum_out=ss[:, c + 1:c + 2]), [u["tg"]] + junk_tok[0])
                    junk_tok[0] = [u["s2"]]

                def s6():
                    c = u["c"]
                    t_m = P.op("dve", lambda E: E.tensor_scalar(out=ss[:, c + 2:c + 3], in0=ss[:, c:c + 1], scalar1=1.0 / 768, scalar2=None, op0=ALU.mult), [u["s1"]])
                    t_q = P.op("dve", lambda E: E.tensor_tensor(out=ss[:, c + 3:c + 4], in0=ss[:, c + 2:c + 3], in1=ss[:, c + 2:c + 3], op=ALU.mult), [t_m])
                    u["tm"] = t_m
                    u["tv"] = P.op("dve", lambda E: E.scalar_tensor_tensor(out=ss[:, c + 3:c + 4], in0=ss[:, c + 1:c + 2], scalar=1.0 / 768, in1=ss[:, c + 3:c + 4],
                                                                            op0=ALU.mult, op1=ALU.subtract), [t_q, u["s2"]])

                def s7():
                    c = u["c"]
                    t_l = P.op("act", lambda E: E.activation(out=ss[:, c + 4:c + 5], in_=ss[:, c + 3:c + 4], func=AF.Ln, bias=EPS), [u["tv"]])
                    u["trs"] = P.op("act", lambda E: E.activation(out=ss[:, c + 4:c + 5], in_=ss[:, c + 4:c + 5], func=AF.Exp, scale=-0.5), [t_l])

                def s8():
                    c = u["c"]
                    u["tn"] = P.op("dve", lambda E: E.scalar_tensor_tensor(out=ss[:, c + 5:c + 6], in0=ss[:, c + 2:c + 3], scalar=-1.0, in1=ss[:, c + 4:c + 5],
                                                                            op0=ALU.mult, op1=ALU.mult), [u["trs"], u["tm"]])

                def s9():
                    c = u["c"]
                    Ta = u["Ta"]
                    u["tvh"] = P.op("act", lambda E: E.activation(out=Ta, in_=Ta, func=AF.Identity, scale=ss[:, c + 4:c + 5], bias=ss[:, c + 5:c + 6]),
                                    [u["tn"], u["trs"], u["s2"], u["s1"]])

                def s10():
                    Ta = u["Ta"]
                    t_a = P.op("pool", lambda E: E.tensor_tensor(out=Ta, in0=Ta, in1=LNG, op=ALU.mult), [u["tvh"], ln_tok])
                    t_b = P.op("dve", lambda E: E.tensor_tensor(out=v_n[:, tt, :], in0=Ta, in1=LNB, op=ALU.add), [t_a, ln_tok] + all_xs_free)
                    vbuf_free[u["ka"]] = [t_b]
                    vn_tok[tt] = t_b
                    ph2_toks.append(t_b)

                return [(0, 6, s0), (0, 8, s1), (1, 1, s1p), (2, 4, s2), (3, 1, s3), (4, 2, s4), (4, 5, s5),
                        (5, 2, s6), (5, 3, s7), (5, 4, s8), (5, 5, s9), (5, 7, s10)]

            USLOT = [11264, 9216, 0, 2048, 7168]
            NUS = len(USLOT)
            uslot_free = [None] * NUS
            hist_tok = [list(setup_toks)]
            ring_ctx = {}
            UNIT0 = TPS + 6

            def slot_free(n):
                k = (n - UNIT0) % NUS
                if uslot_free[k] is None:
                    lo, hi = USLOT[k], USLOT[k] + 2048
                    toks = list(all_xs_free)
                    for j_, vb0 in enumerate(VB):
                        if vb0 < hi and vb0 + 768 > lo:
                            assert vbuf_free[j_] is not None
                            toks += vbuf_free[j_]
                    uslot_free[k] = toks
                return uslot_free[k]

            def slot_views(n):
                base = USLOT[(n - UNIT0) % NUS]
                T3 = R1[:, base:base + 512]
                T0 = R1[:, base + 512:base + 1024]
                T1 = R1[:, base + 1024:base + 1536]
                T2 = R1[:, base + 1536:base + 2048]
                win = R1[:, base + 510:base + 1024]
                return T0, T1, T2, T3, win

            def c_stages(u, n, a_t, blk, tb, T0, T3, off):
                def c_pe():
                    sq_bf = T0[:, 0:256].bitcast(BF16)
                    b, btoks = get_bank()
                    u["cb"] = b
                    u["cmm"] = mm_group(ps[b][:, :], [(onesM[:, :], sq_bf)], [u["tsq"]] + setup_toks, btoks)

                def c_act():
                    b = u["cb"]
                    t_l = P.op("act", lambda E: E.activation(out=T3, in_=ps[b][:, :], func=AF.Ln, bias=EPS), [u["cmm"]])
                    rel(b, [t_l])
                    u["crs"] = P.op("act", lambda E: E.activation(out=T3, in_=T3, func=AF.Exp, scale=-0.5), [t_l])

                def c_dve():
                    dst = headsT[:, blk, tb * 512:(tb + 1) * 512]
                    t_h = P.op("dve", lambda E: E.scalar_tensor_tensor(out=dst, in0=a_t, scalar=gh[:, blk:blk + 1], in1=T3,
                                                                        op0=ALU.mult, op1=ALU.mult), [u["crs"], u["ta"]] + heads_free + setup_toks)
                    uslot_free[(n - UNIT0) % NUS] = [t_h]
                    ph2_toks.append(t_h)
                return [(off, 0, c_pe), (off, 3, c_act), (off, 7, c_dve)]

            def make_u_unit(n, hd, tb):
                u = {}
                T0, T1, T2, T3, win = slot_views(n)
                j, h2 = hd // 2, hd % 2

                def s0():
                    if h2 == 0 and tb == 0:
                        ring_ctx["u"] = ring_take(f"u{st}_{j}")
                    i, slot_t, ltok = ring_ctx["u"]
                    w = wcols(slot_t, KC, 256)
                    b, btoks = get_bank()
                    t_mm = mm_group(ps[b][:, :], [(w[:, kc, h2 * 128:(h2 + 1) * 128], actT[:, kc, tb * 512:(tb + 1) * 512]) for kc in range(KC)],
                                    [ltok] + hT_toks, btoks)
                    u["b"], u["mm"] = b, t_mm
                    if h2 == 1 and tb == 1:
                        ring_release(i, [t_mm])

                def s1():
                    sfree = slot_free(n)
                    b = u["b"]
                    t1 = P.op("act", lambda E: E.activation(out=T0, in_=ps[b][:, :], func=AF.Copy), [u["mm"]] + sfree)
                    rel(b, [t1])
                    u["tx"] = t1

                def s1p():
                    u["ts"] = P.op("pool", lambda E: E.tensor_tensor(out=T1, in0=T0, in1=T0, op=ALU.mult), [u["tx"]] + slot_free(n))

                def s2():
                    t_w = P.op("dve", lambda E: E.tensor_scalar(out=T1, in0=T1, scalar1=K1, scalar2=K2, op0=ALU.mult, op1=ALU.add), [u["ts"]])
                    u["tz"] = P.op("dve", lambda E: E.tensor_tensor(out=T1, in0=T1, in1=T0, op=ALU.mult), [t_w, u["tx"]])

                def s3():
                    u["tr"] = sigmoid_chain(T1, [u["tz"]])
                    b2_, b2toks = get_bank()
                    u["b2"] = b2_
                    P.wait("pe", b2toks)
                    t_sg = None
                    for cidx in range(4):
                        ch = tb * 4 + cidx
                        fn = (lambda ch=ch, cidx=cidx: (lambda E: E.matmul(ps[b2_][:, cidx * 128:(cidx + 1) * 128],
                                                                          lhsT=v_n[:, ch, hd * 128:(hd + 1) * 128], rhs=WsT[:, hd, :], start=True, stop=True)))()
                        t_sg = P.op("pe", fn, [vn_tok[ch]] + setup_toks, sig=(cidx == 3))
                    u["tsg"] = t_sg

                def s4():
                    b2_ = u["b2"]
                    t_g = P.op("dve", lambda E: E.tensor_tensor(out=T1, in0=T0, in1=T1, op=ALU.mult), [u["tr"], u["tx"]])
                    bias_b = biasT[:, hd:hd + 1, :].to_broadcast([128, 4, 128])
                    t_t = P.op("dve", lambda E: E.tensor_tensor(out=T2.rearrange("p (a b) -> p a b", a=4),
                                                                in0=ps[b2_][:, :].rearrange("p (a b) -> p a b", a=4), in1=bias_b, op=ALU.add),
                               [u["tsg"]] + slot_free(n) + setup_toks)
                    rel(b2_, [t_t])
                    u["ta"] = P.op("dve", lambda E: E.tensor_tensor(out=T2, in0=T2, in1=T1, op=ALU.mult), [t_t, t_g])

                def s5():
                    u["tsq"] = P.op("pool", lambda E: E.tensor_tensor(out=T0[:, 0:256].bitcast(BF16), in0=T2, in1=T2, op=ALU.mult), [u["ta"]])

                return [(0, 6, s0), (0, 8, s1), (1, 1, s1p), (2, 4, s2), (3, 1, s3), (4, 2, s4), (4, 5, s5)] + c_stages(u, n, T2, hd, tb, T0, T3, 5)

            def make_c_unit(n, g, tb):
                u = {}
                T0, T1, T2, T3, win = slot_views(n)
                gp, g2 = g // 2, g % 2

                def s0():
                    if g2 == 0 and tb == 0:
                        ring_ctx["cv"] = [ring_take(f"cv{st}_{gp}_{k_}") for k_ in range(3)]
                    cvp = ring_ctx["cv"]
                    bks, mms = [], []
                    for s_ in range(3):
                        w = wcols(cvp[s_][1], KC, 256)
                        b, btoks = get_bank()
                        t_mm = mm_group(ps[b][:, :], [(w[:, kc, g2 * 128:(g2 + 1) * 128], actT[:, kc, tb * 512:(tb + 1) * 512]) for kc in range(KC)],
                                        [cvp[s_][2]] + hT_toks, btoks)
                        bks.append(b)
                        mms.append(t_mm)
                    u["bks"], u["mms"] = bks, mms
                    if g2 == 1 and tb == 1:
                        for c_ in cvp:
                            ring_release(c_[0], [mms[-1]])

                def s1():
                    sfree = slot_free(n)
                    bB, bC, bX = u["bks"]
                    mB, mC, mX = u["mms"]
                    t_c = P.op("act", lambda E: E.activation(out=T1, in_=ps[bC][:, :], func=AF.Copy), [mC] + sfree)
                    rel(bC, [t_c])
                    t_b = P.op("act", lambda E: E.activation(out=T2, in_=ps[bB][:, :], func=AF.Copy), [mB] + sfree)
                    rel(bB, [t_b])
                    u["tb"], u["tc"] = t_b, t_c

                def s1b():
                    sfree = slot_free(n)
                    bB, bC, bX = u["bks"]
                    mB, mC, mX = u["mms"]
                    t_hc = P.op("dve", lambda E: E.tensor_copy(out=T3[:, 510:512], in_=hist[:, 2 * g:2 * g + 2]), sfree + hist_tok[0])
                    t_xc = P.op("dve", lambda E: E.tensor_tensor(out=T0, in0=T1, in1=ps[bX][:, :], op=ALU.mult), [u["tc"], mX] + sfree)
                    rel(bX, [t_xc])
                    t_hs = P.op("dve", lambda E: E.tensor_copy(out=hist[:, 2 * g:2 * g + 2], in_=T0[:, 510:512]), [t_xc, t_hc])
                    hist_tok[0] = [t_hs]
                    u["txc"], u["thc"] = t_xc, t_hc

                def s2():
                    t_y0 = P.op("dve", lambda E: E.tensor_scalar(out=T1, in0=T0, scalar1=cw[:, 3 * g + 2:3 * g + 3], scalar2=None, op0=ALU.mult), [u["txc"], u["thc"]] + setup_toks)
                    t_y1 = P.op("dve", lambda E: E.scalar_tensor_tensor(out=T1, in0=win[:, 1:513], scalar=cw[:, 3 * g + 1:3 * g + 2], in1=T1,
                                                                         op0=ALU.mult, op1=ALU.add), [t_y0])
                    t_y2 = P.op("dve", lambda E: E.scalar_tensor_tensor(out=T1, in0=win[:, 0:512], scalar=cw[:, 3 * g:3 * g + 1], in1=T1,
                                                                         op0=ALU.mult, op1=ALU.add), [t_y1])
                    u["ta"] = P.op("dve", lambda E: E.tensor_tensor(out=T1, in0=T1, in1=T2, op=ALU.mult), [t_y2, u["tb"]])

                def s3():
                    u["tsq"] = P.op("act", lambda E: E.activation(out=T0[:, 0:256].bitcast(BF16), in_=T1, func=AF.Square), [u["ta"]])

                return [(0, 6, s0), (0, 8, s1), (0, 9, s1b), (1, 2, s2), (1, 5, s3)] + c_stages(u, n, T1, 6 + g, tb, T0, T3, 2)

            def make_a_unit(n, hh, tb):
                u = {}
                T0, T1, T2, T3, win = slot_views(n)
                j, h2 = hh // 2, hh % 2
                qT = T0[:, 0:256].bitcast(BF16)
                p_bf = T3.bitcast(BF16)
                pT = T1.bitcast(BF16)
                eT = [T1, T2]

                def s0():
                    if h2 == 0 and tb == 0:
                        ring_ctx["q"] = ring_take(f"q{st}_{j}")
                    i, slot_t, ltok = ring_ctx["q"]
                    w = wcols(slot_t, KC, 256)
                    b, btoks = get_bank()
                    t_mm = mm_group(ps[b][:, :], [(w[:, kc, h2 * 128:(h2 + 1) * 128], actT[:, kc, tb * 512:(tb + 1) * 512]) for kc in range(KC)],
                                    [ltok] + hT_toks, btoks)
                    u["b"], u["mm"] = b, t_mm
                    if h2 == 1 and tb == 1:
                        ring_release(i, [t_mm])

                def s1():
                    b = u["b"]
                    t_q = P.op("act", lambda E: E.activation(out=qT, in_=ps[b][:, :], func=AF.Copy), [u["mm"]] + slot_free(n))
                    rel(b, [t_q])
                    u["tq"] = t_q

                def b1_pe():
                    u["sb"] = []
                    for half in range(2):
                        bs_, bstoks = get_bank()
                        P.wait("pe", bstoks)
                        t_s = None
                        for q4 in range(2):
                            t4 = half * 2 + q4
                            fn = (lambda t4=t4, q4=q4, bs_=bs_: (lambda E: E.matmul(ps[bs_][:, q4 * 256:(q4 + 1) * 256], lhsT=qT[:, t4 * 128:(t4 + 1) * 128],
                                                                                  rhs=KT[:, hh, :], start=True, stop=True)))()
                            t_s = P.op("pe", fn, [u["tq"]] + kv_readers, sig=(q4 == 1))
                        u["sb"].append((bs_, t_s))

                def b1_dve():
                    c = new_stat(12)
                    u["c"] = c
                    u["tnm"] = []
                    for half in range(2):
                        bs_, t_s = u["sb"][half]
                        t_mx = P.op("dve", lambda E, bs_=bs_, half=half: E.tensor_reduce(out=ss[:, c + 2 * half:c + 2 * half + 2],
                                                                                          in_=ps[bs_][:, :].rearrange("p (a b) -> p a b", a=2), axis=AX.X, op=ALU.max), [t_s])
                        t_nm = P.op("dve", lambda E, half=half: E.tensor_scalar(out=ss[:, c + 2 * half:c + 2 * half + 2], in0=ss[:, c + 2 * half:c + 2 * half + 2],
                                                                                 scalar1=-ATT_SCALE, scalar2=None, op0=ALU.mult), [t_mx])
                        u["tnm"].append(t_nm)

                def b1_act():
                    sfree = slot_free(n)
                    c = u["c"]
                    te = []
                    for half in range(2):
                        bs_, t_s = u["sb"][half]
                        for q4 in range(2):
                            t4 = half * 2 + q4
                            tl = P.op("act", lambda E, bs_=bs_, q4=q4, t4=t4, half=half: E.activation(out=eT[half][:, q4 * 256:(q4 + 1) * 256], in_=ps[bs_][:, q4 * 256:(q4 + 1) * 256],
                                                                                                    func=AF.Exp, scale=ATT_SCALE, bias=ss[:, c + t4:c + t4 + 1],
                                                                                                    accum_out=ss[:, c + 4 + t4:c + 5 + t4]), [u["tnm"][half], t_s] + sfree)
                            te.append(tl)
                        rel(bs_, te[-2:])
                    u["te"] = te

                def b2a():
                    c = u["c"]
                    t_rz = P.op("dve", lambda E: E.reciprocal(out=ss[:, c + 8:c + 12], in_=ss[:, c + 4:c + 8]), u["te"])
                    tp = []
                    for t4 in range(4):
                        half, q4 = t4 // 2, t4 % 2
                        tp.append(P.op("dve", lambda E, t4=t4, half=half, q4=q4: E.tensor_scalar(out=p_bf[:, t4 * 256:(t4 + 1) * 256], in0=eT[half][:, q4 * 256:(q4 + 1) * 256],
                                                                                                  scalar1=ss[:, c + 8 + t4:c + 9 + t4], scalar2=None, op0=ALU.mult), [t_rz] + u["te"] + slot_free(n)))
                    u["tp"] = tp

                def b2b_pe():
                    tp = u["tp"]
                    bt, bttoks = get_bank()
                    ptv = ps[bt][:, :].bitcast(BF16)
                    P.wait("pe", bttoks)
                    t_tr = None
                    for t4 in range(4):
                        for mc in range(2):
                            k8 = t4 * 2 + mc
                            fn = (lambda t4=t4, mc=mc, k8=k8: (lambda E: E.transpose(ptv[:, k8 * 128:(k8 + 1) * 128], p_bf[:, t4 * 256 + mc * 128:t4 * 256 + (mc + 1) * 128], ident[:, :])))()
                            t_tr = P.op("pe", fn, tp + setup_toks, sig=(k8 == 7))
                    u["bt"], u["ptv"], u["ttr"] = bt, ptv, t_tr

                def b2b_act():
                    t_pt = P.op("act", lambda E: E.activation(out=pT, in_=u["ptv"], func=AF.Copy), [u["ttr"]] + u["tp"])
                    rel(u["bt"], [t_pt])
                    u["tpt"] = t_pt

                def b3_pe():
                    bo, botoks = get_bank()
                    P.wait("pe", botoks)
                    t_pv = None
                    for t4 in range(4):
                        for mc in range(2):
                            k8 = t4 * 2 + mc
                            fn = (lambda t4=t4, mc=mc, k8=k8: (lambda E: E.matmul(ps[bo][:, t4 * 128:(t4 + 1) * 128], lhsT=Vm[:, mc, hh * 128:(hh + 1) * 128],
                                                                                rhs=pT[:, k8 * 128:(k8 + 1) * 128], start=(mc == 0), stop=(mc == 1))))()
                            t_pv = P.op("pe", fn, [u["tpt"]] + kv_readers, sig=(k8 == 7))
                    u["bo"], u["tpv"] = bo, t_pv

                def b3_act():
                    bo = u["bo"]
                    t_o = P.op("act", lambda E: E.activation(out=T2, in_=ps[bo][:, :], func=AF.Copy), [u["tpv"], u["tpt"]])
                    rel(bo, [t_o])
                    u["ta"] = t_o
                    u["tsq"] = P.op("pool", lambda E: E.tensor_tensor(out=T0[:, 0:256].bitcast(BF16), in0=T2, in1=T2, op=ALU.mult), [t_o])

                return [(0, 6, s0), (0, 8, s1), (1, 0, b1_pe), (1, 2, b1_dve), (1, 5, b1_act), (2, 2, b2a),
                        (3, 0, b2b_pe), (3, 3, b2b_act), (4, 0, b3_pe), (4, 3, b3_act)] + c_stages(u, n, T2, 12 + hh, tb, T0, T3, 5)

            def make_kv_units():
                Ms = [R1[:, 11264:13312], headsT[:, 0:4, :].bitcast(F32).rearrange("p a b -> p (a b)")]
                memT = headsT[:, 4:8, :].rearrange("p a (b c) -> p (a b) c", c=256)
                m_slots = [xs_slots[2], xs_slots[3]]
                st8 = kv_state
                memT_toks = []

                def mk(mt):
                    u = {}

                    def s_dma():
                        if mt == 0:
                            st8["tG"] = P.dma("sp", Gt[:, :], g_mem_d.partition_broadcast(128), g_slot, g_last_use)
                        u["tx"] = P.dma("sp", Ms[mt], mem_d[mt * 128:(mt + 1) * 128, :], m_slots[mt], all_xs_free)
                    s_act, s_dve = norm_unit(u, Ms[mt], Gt[:, :], mt, lambda: [u["tx"], st8["tG"]] + setup_toks, st8.setdefault("g_use", []))

                    def s_pe():
                        pe_toks, ev = transpose16(hb[mt], lambda half: memT[:, half * 8:(half + 1) * 8, mt * 128:(mt + 1) * 128],
                                                  [u["th"]] + setup_toks, defer=True)
                        hb_free[mt] = list(pe_toks)
                        u["ev"] = ev

                    def s_ev():
                        memT_toks.extend(u["ev"]())
                        if mt == 1:
                            emit_kv()
                    return [(0, 9, s_dma), (1, 3, s_act), (2, 2, s_dve), (3, 0, s_pe), (4, 8, s_ev)]

                def emit_kv():
                    last_all = []
                    for j in range(4):
                        i, slot_t, ltok = ring_take(f"kv{j}")
                        w = wcols(slot_t, KC, 256)
                        last_pe = []
                        if j < 2:
                            for hh2 in range(2):
                                hh = j * 2 + hh2
                                b, btoks = get_bank()
                                t_mm = mm_group(ps[b][:, 0:256], [(w[:, kc, hh2 * 128:(hh2 + 1) * 128], memT[:, kc, :]) for kc in range(KC)],
                                                [ltok] + memT_toks, btoks)
                                t_c = P.op("act", lambda E, b=b, hh=hh: E.activation(out=KT[:, hh, :], in_=ps[b][:, 0:256], func=AF.Copy), [t_mm])
                                rel(b, [t_c])
                                last_pe.append(t_mm)
                                kv_readers.append(t_c)
                        else:
                            for mc in range(2):
                                b, btoks = get_bank()
                                t_mm = mm_group(ps[b][:, 0:256], [(memT[:, kc, mc * 128:(mc + 1) * 128], w[:, kc, :]) for kc in range(KC)],
                                                [ltok] + memT_toks, btoks)
                                t_c = P.op("act", lambda E, b=b, mc=mc, j=j: E.activation(out=Vm[:, mc, (j - 2) * 256:(j - 1) * 256], in_=ps[b][:, 0:256], func=AF.Copy), [t_mm])
                                rel(b, [t_c])
                                last_pe.append(t_mm)
                                kv_readers.append(t_c)
                        ring_release(i, last_pe)
                        last_all = last_pe
                    heads_free.extend(kv_readers)
                    uslot_free[0] = list(kv_readers) + list(all_xs_free)
                return [mk(0), mk(1)]

            kv_state = {}
            kv_units = make_kv_units() if st == 0 else [[], []]
            units = [make_v_unit(tt) for tt in range(TPS)] + kv_units + [[], [], [], []]
            for hh in range(4):
                for tb in range(2):
                    units.append(make_a_unit(len(units), hh, tb))
            for hd in range(6):
                for tb in range(2):
                    units.append(make_u_unit(len(units), hd, tb))
            for g in range(6):
                for tb in range(2):
                    units.append(make_c_unit(len(units), g, tb))
            run_sched(units)
            g_last_use = list(g_last_use) + list(kv_state.get("g_use", []))
            vn_all = [vn_tok[tt] for tt in range(TPS)]
            heads_toks = list(ph2_toks)
            heads_free = []
            ph2_end = [t for sl_ in uslot_free if sl_ for t in sl_] + vn_all[-1:]

            xr_toks = []
            for tt in range(TPS):
                xr_toks.append(P.dma("sp", xres[:, tt, :], x_d[tok0 + tt * 128: tok0 + (tt + 1) * 128, :], xr_slots[tt],
                                     heads_toks[-1:] + ph2_end + hT_toks[-2:] + list(prev_st_done)))
            tGf = P.dma("sp", Gt[:, :], g_ffn_d.partition_broadcast(128), g_slot, g_last_use)
            g_last_use = []
            x1_toks = [[] for _ in range(TPS)]
            wo_last = []

            def wo_group(j, tt, w, ltok):
                b, btoks = get_bank()
                t_mm = mm_group(ps[b][:, 0:256], [(headsT[:, kc, tt * 128:(tt + 1) * 128], w[:, kc, :]) for kc in range(KC)],
                                [ltok] + heads_toks, btoks)
                dst = xres[:, tt, j * 256:(j + 1) * 256]
                t_add = P.op("dve", lambda E: E.tensor_tensor(out=dst, in0=dst, in1=ps[b][:, 0:256], op=ALU.add), [t_mm, xr_toks[tt]])
                rel(b, [t_add])
                x1_toks[tt].append(t_add)
                return t_mm

            for j in range(7):
                i, slot_t, ltok = ring_take(f"wo{st}_{j}")
                w = wcols(slot_t, KC, 256)
                for tt in range(TPS):
                    wo_last = [wo_group(j, tt, w, ltok)]
                ring_release(i, wo_last)
            i7, slot7, ltok7 = ring_take(f"wo{st}_7")
            w7 = wcols(slot7, KC, 256)
            n2_toks = []
            g_use = []
            wo_fin = {}

            def make_p3_unit(tt):
                u = {}
                hsl = tt % 2

                def s_mm():
                    wo_fin[tt] = wo_group(7, tt, w7, ltok7)
                    if tt == TPS - 1:
                        ring_release(i7, [wo_fin[tt]])
                s_act, s_dve = norm_unit(u, xres[:, tt, :], Gt[:, :], hsl, lambda: x1_toks[tt][-1:] + [tGf], g_use)

                def s_pe():
                    pe_toks, ev = transpose16(hb[hsl], lambda half: actT[:, half * 8:(half + 1) * 8, tt * 128:(tt + 1) * 128],
                                              [u["th"]] + heads_toks[-1:] + ph2_end, defer=True)
                    hb_free[hsl] = list(pe_toks)
                    u["ev"] = ev

                def s_ev():
                    n2_toks.extend(u["ev"]())
                return [(0, 0, s_mm), (1, 1, s_act), (2, 1, s_dve), (3, 1, s_pe), (4, 1, s_ev)]

            run_sched([make_p3_unit(tt) for tt in range(TPS)])
            wo_last = [wo_fin[TPS - 1]]
            g_last_use = g_use

            nfc = DFF // 512
            g_ready = [[], []]
            g_readers = [list(wo_last), list(wo_last)]
            x2_last = [x1_toks[tt][-1:] for tt in range(TPS)]
            rs_free = [[], []]
            rs_i = [0]

            def ffn1(fc):
                buf = fc % 2
                toks = []
                for p_ in range(2):
                    i, slot_t, ltok = ring_take(f"w1_{st}_{fc}_{p_}")
                    w = wcols(slot_t, KC, 256)
                    last_pe = []
                    for fb in range(2):
                        for tb in range(2):
                            b, btoks = get_bank()
                            t_mm = mm_group(ps[b][:, :], [(w[:, kc, fb * 128:(fb + 1) * 128], actT[:, kc, tb * 512:(tb + 1) * 512]) for kc in range(KC)],
                                            [ltok] + n2_toks, btoks)
                            last_pe = [t_mm]
                            k = rs_i[0] % 2
                            rs_i[0] += 1
                            t_r = P.op("act", lambda E, b=b, k=k: E.activation(out=rsc[k], in_=ps[b][:, :], func=AF.Relu), [t_mm] + rs_free[k])
                            rel(b, [t_r])
                            dst = gT[buf][:, p_ * 2 + fb, tb * 512:(tb + 1) * 512]
                            t_g = P.op("pool", lambda E, k=k, dst=dst: E.tensor_tensor(out=dst, in0=rsc[k], in1=rsc[k], op=ALU.mult), [t_r] + g_readers[buf])
                            rs_free[k] = [t_g]
                            toks.append(t_g)
                    ring_release(i, last_pe)
                g_readers[buf] = []
                g_ready[buf] = toks

            def ffn2(fc, tile_hook=None):
                buf = fc % 2
                pcs = [ring_take(f"w2_{st}_{fc}_{p_}") for p_ in range(2)]
                ws = [wcols(pc[1], 2, D) for pc in pcs]
                last_pe = []
                for tt in range(TPS):
                    if tile_hook is not None:
                        tile_hook(tt)
                    for cg in range(4):
                        b, btoks = get_bank()
                        pairs = [(gT[buf][:, fbk, tt * 128:(tt + 1) * 128], ws[fbk // 2][:, fbk % 2, cg * 512:(cg + 1) * 512]) for fbk in range(4)]
                        t_mm = mm_group(ps[b][:, :], pairs, [pcs[0][2], pcs[1][2]] + g_ready[buf], btoks)
                        last_pe = [t_mm]
                        dst = xres[:, tt, cg * 512:(cg + 1) * 512]
                        t_add = P.op("dve", lambda E, b=b, dst=dst: E.tensor_tensor(out=dst, in0=dst, in1=ps[b][:, :], op=ALU.add), [t_mm])
                        rel(b, [t_add])
                        x2_last[tt] = [t_add]
                g_readers[buf] = list(last_pe)
                for pc in pcs:
                    ring_release(pc[0], last_pe)
                return last_pe

            tGo = P.dma("sp", Gt[:, :], g_fin_d.partition_broadcast(128), g_slot, g_last_use)
            g_last_use = []
            g_use = []
            st_store = {}

            p5 = {}

            def p5_act(tt):
                u = p5.setdefault(tt, {})
                src = xres[:, tt, :]
                c = new_stat(2)
                u["c"] = c
                t0 = P.op("act", lambda E: E.activation(out=junk[:, :], in_=src, func=AF.Square, accum_out=ss[:, c:c + 1]), x2_last[tt] + junk_tok[0])
                junk_tok[0] = [t0]
                u["t0"] = t0
                u["t1"] = rstd_from_sum(c, c + 1, 1.0 / D, [t0])

            def p5_dve(tt):
                u = p5[tt]
                src = xres[:, tt, :]
                c = u["c"]
                t2 = P.op("dve", lambda E: E.scalar_tensor_tensor(out=src, in0=src, scalar=ss[:, c + 1:c + 2], in1=Gt[:, :],
                                                                   op0=ALU.mult, op1=ALU.mult), [u["t1"], u["t0"], tGo])
                g_use.append(t2)
                u["t2"] = t2

            def p5_out(tt):
                src = xres[:, tt, :]
                t3 = P.dma("sp", out_d[tok0 + tt * 128: tok0 + (tt + 1) * 128, :], src, o_slots[tt], [p5[tt]["t2"]])
                out_toks.append(t3)
                st_store[tt] = t3

            def p5_hook(tt):
                if tt - 1 >= 0:
                    p5_act(tt - 1)
                if tt - 2 >= 0:
                    p5_dve(tt - 2)
                if tt - 3 >= 0:
                    p5_out(tt - 3)

            ffn1(0)
            f2_last = []
            for fc in range(nfc):
                if fc + 1 < nfc:
                    ffn1(fc + 1)
                f2_last = ffn2(fc, p5_hook if fc == nfc - 1 else None)
            heads_free = list(f2_last)
            actT_free = list(f2_last)
            p5_act(TPS - 1)
            p5_dve(TPS - 2)
            p5_out(TPS - 3)
            p5_dve(TPS - 1)
            p5_out(TPS - 2)
            p5_out(TPS - 1)
            g_last_use = g_use
            prev_st_done = [st_store[tt] for tt in range(TPS)]
            prev_store = dict(st_store)
            r1_free = []

        P.wait("sp", out_toks)
        assert ring_state["taken"] == len(pieces), (ring_state, len(pieces))

        with nc.Block() as block:
            @block.tensor
            def _(E):
                for f in P.q["pe"]:
                    f(E)

            @block.scalar
            def _(E):
                for f in P.q["act"]:
                    f(E)

            @block.vector
            def _(E):
                for f in P.q["dve"]:
                    f(E)

            @block.gpsimd
            def _(E):
                for f in P.q["pool"]:
                    f(E)

            @block.sync
            def _(E):
                for f in P.q["sp"]:
                    f(E)
    return nc


_NC_CACHE = {}


def kernel(x, mem, g_mix, w_in, ln_v_g, ln_v_b, w_s, b_s, conv_w, g_mem, w_kv, g_head, w_o, g_ffn,
           w_ffn1, w_ffn2, g_final):
    f = lambda a: np.ascontiguousarray(np.asarray(a, dtype=np.float32))
    x = f(x)
    mem = f(mem)
    shared = {
        "g_mix": f(g_mix).reshape(D),
        "w_in": f(w_in).reshape(D, D_IN),
        "ln_v_g": f(ln_v_g).reshape(768),
        "ln_v_b": f(ln_v_b).reshape(768),
        "w_sT": f(np.asarray(w_s).reshape(6, 128, 128).transpose(0, 2, 1)),
        "b_s": f(b_s).reshape(768),
        "conv_wp": f(np.asarray(conv_w).reshape(3, 6, 128).transpose(2, 1, 0).reshape(128, 18)),
        "g_mem": f(g_mem).reshape(D),
        "w_kv": f(w_kv).reshape(D, 1024),
        "g_headp": f(np.asarray(g_head).reshape(16, 128).T),
        "w_o": f(w_o).reshape(D, D),
        "g_ffn": f(g_ffn).reshape(D),
        "w_ffn1": f(w_ffn1).reshape(D, DFF),
        "w_ffn2": f(w_ffn2).reshape(DFF, D),
        "g_final": f(g_final).reshape(D),
    }
    if "nc" not in _NC_CACHE:
        _NC_CACHE["nc"] = build_nc()
    nc = _NC_CACHE["nc"]
    in_maps = []
    for b in range(8):
        m = dict(shared)
        m["x"] = x[b]
        m["mem"] = mem[b]
        in_maps.append(m)
    res = run_bass_kernel_spmd(nc, in_maps, core_ids=list(range(8)))
    out = np.stack([np.asarray(r["out"], dtype=np.float32) for r in res.results], axis=0)
    return out
```
